# Optimizing a Trainium2 kernel written in Bass

```python
import math
import jax, jax.numpy as jnp
from jax import lax
import numpy as np

D_MODEL = 2048
BATCH = 8
SEQ = 2048
DEPTH = 1

CHUNK = 64
RNN_WIDTH = 1024
N_RNN_HEADS = 16
RNN_HEAD_DIM = RNN_WIDTH // N_RNN_HEADS
RNN_CONV_W = 4
C_RG = 8.0
N_SB_HEADS = 8
SB_HEAD_DIM = 128
SB_WIDTH = N_SB_HEADS * SB_HEAD_DIM
Q_BLOCK = 128
D_FF = 3 * D_MODEL
FFN_CONV_W = 3
EPS = 1e-6
IN_SPLITS = (RNN_WIDTH, RNN_WIDTH, SB_WIDTH, SB_WIDTH, SB_WIDTH, D_MODEL, D_MODEL)
IN_COLS = sum(IN_SPLITS)

kernel_name = "hawk_stickbreak_parallel_hybrid"


def rms_norm(x, g):
    xf = x.astype(jnp.float32)
    y = xf * lax.rsqrt(jnp.mean(xf * xf, axis=-1, keepdims=True) + EPS)
    return (y * g.astype(jnp.float32)).astype(x.dtype)


def causal_depthwise_conv(x, w, b):
    k_w, ch = w.shape
    y = lax.conv_general_dilated(
        x, w[:, None, :].astype(x.dtype), window_strides=(1,),
        padding=[(k_w - 1, 0)], dimension_numbers=("NWC", "WIO", "NWC"),
        feature_group_count=ch)
    return y + b.astype(x.dtype)


def rg_lru(xr, w_a, b_a, w_x, b_x, lam):
    bsz, seq, ch = xr.shape
    xh = xr.reshape(bsz, seq, N_RNN_HEADS, RNN_HEAD_DIM)
    r = jax.nn.sigmoid(jnp.einsum('bshi,hij->bshj', xh, w_a).reshape(bsz, seq, ch) + b_a)
    i = jax.nn.sigmoid(jnp.einsum('bshi,hij->bshj', xh, w_x).reshape(bsz, seq, ch) + b_x)
    log_a = -C_RG * r.astype(jnp.float32) * jax.nn.softplus(-lam.astype(jnp.float32))
    a = jnp.exp(log_a)
    b = jnp.sqrt(-jnp.expm1(2.0 * log_a)) * (i * xr).astype(jnp.float32)

    def combine(left, right):
        a_l, b_l = left
        a_r, b_r = right
        return a_l * a_r, a_r * b_l + b_r

    _, h = lax.associative_scan(combine, (a, b), axis=1)
    return h.astype(xr.dtype)


def stick_breaking_attention(q, k, v):
    bsz, seq, n_h, d_h = q.shape
    scale = d_h ** -0.5
    outs = []
    for blk in range(seq // Q_BLOCK):
        q0 = blk * Q_BLOCK
        kv_len = q0 + Q_BLOCK
        qb = q[:, q0:kv_len].astype(jnp.float32)
        kb = k[:, :kv_len].astype(jnp.float32)
        vb = v[:, :kv_len].astype(jnp.float32)
        z = jnp.einsum('bqhd,bkhd->bhqk', qb, kb) * scale
        t_idx = q0 + jnp.arange(Q_BLOCK)[:, None]
        s_idx = jnp.arange(kv_len)[None, :]
        mask = s_idx < t_idx
        log_keep = jnp.where(mask, jax.nn.log_sigmoid(-z), 0.0)
        between = lax.cumsum(log_keep, axis=3, reverse=True) - log_keep
        w = jnp.where(mask, jnp.exp(jax.nn.log_sigmoid(z) + between), 0.0)
        outs.append(jnp.einsum('bhqk,bkhd->bqhd', w, vb))
    return jnp.concatenate(outs, axis=1).astype(q.dtype)


def setup_inputs(seed: int = 0) -> dict:
    key = jax.random.key(seed)
    ks = jax.random.split(key, 24)
    f32 = jnp.float32

    def nrm(k, shape, fan_in):
        return jax.random.normal(k, shape, f32) * (fan_in ** -0.5)

    def gain(k, shape):
        return 1.0 + 0.02 * jax.random.normal(k, shape, f32)

    def bias(k, shape):
        return 0.02 * jax.random.normal(k, shape, f32)

    L = DEPTH
    u = jax.random.uniform(ks[12], (L, RNN_WIDTH), f32, 0.9, 0.999)
    root = u ** (1.0 / C_RG)
    lru_lambda = jnp.log(root) - jnp.log1p(-root)
    return {
        "x": jax.random.normal(ks[0], (BATCH, SEQ, D_MODEL), f32),
        "c": jax.random.normal(ks[1], (BATCH, D_MODEL), f32),
        "w_ada": nrm(ks[2], (L, D_MODEL, 6 * D_MODEL), D_MODEL),
        "b_ada": bias(ks[3], (L, 6 * D_MODEL)),
        "g_norm1": gain(ks[4], (L, D_MODEL)),
        "w_in": nrm(ks[5], (L, D_MODEL, IN_COLS), D_MODEL),
        "conv_rnn_w": nrm(ks[6], (L, RNN_CONV_W, RNN_WIDTH), RNN_CONV_W),
        "conv_rnn_b": bias(ks[7], (L, RNN_WIDTH)),
        "w_rg_a": nrm(ks[8], (L, N_RNN_HEADS, RNN_HEAD_DIM, RNN_HEAD_DIM), RNN_HEAD_DIM),
        "b_rg_a": bias(ks[9], (L, RNN_WIDTH)),
        "w_rg_x": nrm(ks[10], (L, N_RNN_HEADS, RNN_HEAD_DIM, RNN_HEAD_DIM), RNN_HEAD_DIM),
        "b_rg_x": bias(ks[11], (L, RNN_WIDTH)),
        "lru_lambda": lru_lambda,
        "g_q": gain(ks[13], (L, SB_HEAD_DIM)),
        "g_k": gain(ks[14], (L, SB_HEAD_DIM)),
        "w_proj_rnn": nrm(ks[15], (L, RNN_WIDTH, D_MODEL), RNN_WIDTH),
        "w_proj_sb": nrm(ks[16], (L, SB_WIDTH, D_MODEL), SB_WIDTH),
        "w_out": nrm(ks[17], (L, D_MODEL, D_MODEL), D_MODEL),
        "g_norm2": gain(ks[18], (L, D_MODEL)),
        "w_up": nrm(ks[19], (L, D_MODEL, 2 * D_FF), D_MODEL),
        "conv_ffn_w": nrm(ks[20], (L, FFN_CONV_W, D_FF), FFN_CONV_W),
        "conv_ffn_b": bias(ks[21], (L, D_FF)),
        "w_down": nrm(ks[22], (L, D_FF, D_MODEL), D_FF),
    }


def reference(x, c, w_ada, b_ada, g_norm1, w_in, conv_rnn_w, conv_rnn_b, w_rg_a, b_rg_a,
              w_rg_x, b_rg_x, lru_lambda, g_q, g_k, w_proj_rnn, w_proj_sb, w_out,
              g_norm2, w_up, conv_ffn_w, conv_ffn_b, w_down):
    bsz, seq, _ = x.shape
    split_idx = list(np.cumsum(IN_SPLITS)[:-1])
    for l in range(DEPTH):
        mod = jax.nn.silu(c) @ w_ada[l] + b_ada[l]
        shift1, scale1, gate1, shift2, scale2, gate2 = [m[:, None, :] for m in jnp.split(mod, 6, axis=-1)]

        h = rms_norm(x, g_norm1[l]) * (1.0 + scale1) + shift1
        p = h @ w_in[l]
        x_rnn, gate_rnn, q, k, v, gm_rnn, gm_sb = jnp.split(p, split_idx, axis=-1)

        xr = causal_depthwise_conv(x_rnn, conv_rnn_w[l], conv_rnn_b[l])
        hr = rg_lru(xr, w_rg_a[l], b_rg_a[l], w_rg_x[l], b_rg_x[l], lru_lambda[l])
        y_rnn = (jax.nn.gelu(gate_rnn) * hr) @ w_proj_rnn[l]

        qh = rms_norm(q.reshape(bsz, seq, N_SB_HEADS, SB_HEAD_DIM), g_q[l])
        kh = rms_norm(k.reshape(bsz, seq, N_SB_HEADS, SB_HEAD_DIM), g_k[l])
        vh = v.reshape(bsz, seq, N_SB_HEADS, SB_HEAD_DIM)
        o_sb = stick_breaking_attention(qh, kh, vh).reshape(bsz, seq, SB_WIDTH)
        y_sb = o_sb @ w_proj_sb[l]

        merged = jax.nn.sigmoid(gm_rnn) * y_rnn + jax.nn.sigmoid(gm_sb) * y_sb
        x = x + gate1 * (merged @ w_out[l])

        h2 = rms_norm(x, g_norm2[l]) * (1.0 + scale2) + shift2
        up_val, up_gate = jnp.split(h2 @ w_up[l], 2, axis=-1)
        up_gate = causal_depthwise_conv(up_gate, conv_ffn_w[l], conv_ffn_b[l])
        x = x + gate2 * ((jax.nn.gelu(up_gate) * up_val) @ w_down[l])
    return x
```

```python
import numpy as np
from contextlib import ExitStack
import concourse.bass as bass
import concourse.mybir as mybir
from concourse.bass_utils import run_bass_kernel_spmd

F32 = mybir.dt.float32
BF16 = mybir.dt.bfloat16
U8 = mybir.dt.uint8
AF = mybir.ActivationFunctionType
ALU = mybir.AluOpType

S = 2048
D = 2048
NKC = 16
DFF = 6144
EPS = 1e-6
ARENA_BYTES = 212000


class Chan:
    def __init__(self, sem, name):
        self.sem = sem
        self.n = 0
        self.name = name


class Prog:
    ENG = ('pe', 'act', 'dve', 'pool', 'sp')

    def __init__(self, nc, stack):
        self.nc = nc
        self.stack = stack
        self.q = {e: [] for e in self.ENG}
        self.done = {e: Chan(stack.enter_context(nc.semaphore("done_" + e)), "done_" + e) for e in self.ENG}
        self.waited = {e: {} for e in self.ENG}
        self.nchan = 0

    def dma_chan(self, name):
        self.nchan += 1
        return Chan(self.stack.enter_context(self.nc.semaphore(name)), name)

    def _filter(self, eng, wait):
        ws = []
        for tok in wait:
            if tok is None:
                continue
            ch, v = tok
            if self.waited[eng].get(ch.name, 0) >= v:
                continue
            self.waited[eng][ch.name] = v
            ws.append((ch, v))
        return ws

    def op(self, eng, fn, wait=(), sig=True):
        ws = self._filter(eng, wait)
        tok = None
        if sig:
            ch = self.done[eng]
            ch.n += 1
            tok = (ch, ch.n)
        dch = self.done[eng]

        def thunk(e, ws=ws, fn=fn, sig=sig, dch=dch):
            for ch, v in ws:
                e.wait_ge(ch.sem, v)
            ins = fn(e)
            if sig:
                ins.then_inc(dch.sem, 1)
        self.q[eng].append(thunk)
        return tok

    def dma(self, eng, chan, out, in_, wait=(), slow=False):
        ws = self._filter(eng, wait)
        chan.n += 16
        tok = (chan, chan.n)

        def thunk(e, ws=ws, out=out, in_=in_, chan=chan, slow=slow):
            for ch, v in ws:
                e.wait_ge(ch.sem, v)
            if slow:
                e.dma_start(out=out, in_=in_, allow_slow_non_contiguous=True).then_inc(chan.sem, 16)
            else:
                e.dma_start(out=out, in_=in_).then_inc(chan.sem, 16)
        self.q[eng].append(thunk)
        return tok

    def barrier(self, engs, chans=()):
        toks = [(self.done[e], self.done[e].n) for e in ('pe', 'act', 'dve') if self.done[e].n > 0]
        toks += [(ch, ch.n) for ch in chans if ch.n > 0]
        for e in engs:
            self.wait_only(e, toks)

    def wait_only(self, eng, wait):
        for ch, v in self._filter(eng, wait):
            self.q[eng].append(lambda e, ch=ch, v=v: e.wait_ge(ch.sem, v))

    def run(self, block):
        q = self.q

        @block.tensor
        def _(e):
            for t in q['pe']:
                t(e)

        @block.scalar
        def _(e):
            for t in q['act']:
                t(e)

        @block.vector
        def _(e):
            for t in q['dve']:
                t(e)

        @block.gpsimd
        def _(e):
            for t in q['pool']:
                t(e)

        @block.sync
        def _(e):
            for t in q['sp']:
                t(e)


class Arena:
    def __init__(self, ap_u8, size):
        self.ap = ap_u8
        self.size = size
        self.off = 0

    def alloc(self, shape, dt, off=None):
        n = int(np.prod(shape[1:])) * mybir.dt.size(dt)
        if off is None:
            off = self.off
            self.off += (n + 63) // 64 * 64
            assert self.off <= self.size, f"arena overflow {self.off} > {self.size}"
        else:
            assert off + n <= self.size, f"arena overflow (explicit) {off + n} > {self.size}"
        ap = self.ap[:, off:off + n].bitcast(dt)
        if len(shape) == 3:
            ap = ap.rearrange("p (a b) -> p a b", a=shape[1])
        if shape[0] < 128:
            ap = ap[0:shape[0]]
        return ap


class Ring:
    def __init__(self, aps, chans=None):
        self.aps = aps
        self.n = len(aps)
        self.free = [[] for _ in aps]
        self.chans = chans
        self.i = 0

    def acquire(self):
        i = self.i % self.n
        self.i += 1
        return i, self.aps[i], list(self.free[i])

    def release(self, i, toks):
        self.free[i] = [t for t in toks if t is not None]


def build_nc(dbg=(), stop_after=None):
    nc = bass.Bass("TRN2", target_bir_lowering=False)
    t = {}

    def din(name, shape, dt=F32):
        t[name] = nc.dram_tensor(name, shape, dt, kind="ExternalInput").ap()

    din("x", [S, D]); din("cT", [128, 16]); din("w_ada", [D, 6 * D]); din("b_ada", [1, 6 * D])
    din("g1col", [128, 16]); din("g2col", [128, 16]); din("w_in", [D, 9216])
    din("rnn_cols", [128, 64]); din("wa_bd", [128, 8, 128]); din("wx_bd", [128, 8, 128])
    din("gqk", [128, 2]); din("w_proj_rnn", [1024, D]); din("w_proj_sb", [1024, D]); din("w_out", [D, D])
    din("w_up", [D, 2 * DFF]); din("w_down", [DFF, D]); din("ffn_cols", [128, 192]); din("consts", [128, 4, 128])
    out = nc.dram_tensor("out", [S, D], F32, kind="ExternalOutput").ap()
    sg = nc.dram_tensor("sg_scr", [32, 128, S], F32).ap()
    mT = nc.dram_tensor("mT_scr", [16, 128, S], BF16).ap()
    gate_scr = nc.dram_tensor("gate_scr", [2, 128, D], F32).ap()
    col_scr = nc.dram_tensor("col_scr", [4, D], F32).ap()
    dbg_out = {}
    dbg_shapes = {"A1col": [128, 16], "B1col": [128, 16], "A2col": [128, 16], "B2col": [128, 16],
                  "hT": [128, 16 * S], "gT": [128, 8 * S], "oT": [128, 8 * S], "g1bc": [128, D], "g2bc": [128, D],
                  "qT0": [128, S], "kT0": [128, S], "v0": [128, S]}
    for name in dbg:
        dt_ = F32 if name in ("A1col", "B1col", "A2col", "B2col", "g1bc", "g2bc") else BF16
        dbg_out[name] = nc.dram_tensor("dbg_" + name, dbg_shapes[name], dt_, kind="ExternalOutput").ap()

    with ExitStack() as st:
        arena_t = st.enter_context(nc.sbuf_tensor("arena", [128, ARENA_BYTES], U8))
        ps_t = st.enter_context(nc.psum_tensor("ps", [128, 8, 512], F32))
        P = Prog(nc, st)
        A = Arena(arena_t, ARENA_BYTES)
        op = P.op

        def bank(b):
            return ps_t[:, b, :]
        bank_free = [[] for _ in range(8)]

        def mm_group(out_ap, pairs, waits=(), sig=True):
            n = len(pairs)
            tok = None
            for i, (l, r) in enumerate(pairs):
                tok = op('pe', lambda e, l=l, r=r, i=i: e.matmul(out_ap, lhsT=l, rhs=r, start=(i == 0), stop=(i == n - 1)),
                         wait=(waits if i == 0 else ()), sig=(sig and i == n - 1))
            return tok

        class BankRing:
            def __init__(self, ids):
                self.ids = ids
                self.i = 0

            def acquire(self):
                b = self.ids[self.i % len(self.ids)]
                self.i += 1
                return b, list(bank_free[b])

        cst_f = A.alloc([128, 4, 128], F32)
        identf = cst_f[:, 0, :]
        masku_f = cst_f[:, 3, :]
        identb = A.alloc([128, 128], BF16)
        tgeb = A.alloc([128, 128], BF16)
        onesb = A.alloc([128, 128], BF16)
        maskub = A.alloc([128, 128], BF16)
        onesnb = A.alloc([128, 128], BF16)
        SC = A.alloc([128, 16, 128], BF16)
        cTt = A.alloc([128, 16], F32); scol = A.alloc([128, 16], F32)
        g1c = A.alloc([128, 16], F32); g2c = A.alloc([128, 16], F32)
        modcols = [A.alloc([128, 16], F32) for _ in range(4)]
        A1col = A.alloc([128, 16], F32); A2col = A.alloc([128, 16], F32)
        B1col = modcols[0]; B2col = modcols[2]
        rnnc = A.alloc([128, 64], F32)
        rnnd = A.alloc([128, 32], F32)
        rtmp = A.alloc([128, 16], F32)
        gqk = A.alloc([128, 2], F32); gqs = A.alloc([128, 1], F32)
        ffnc = A.alloc([128, 192], F32)
        halo = A.alloc([128, 96], F32)
        small = [A.alloc([128, 4], F32) for _ in range(4)]
        RING_OFF = A.off
        A.off += 65536
        BIG_OFF = A.off
        BIG_SIZE = ARENA_BYTES - BIG_OFF
        ring32 = Ring([A.alloc([128, 16, 1024], BF16, off=RING_OFF + i * 32768) for i in range(2)])
        ring16 = Ring([A.alloc([128, 16, 512], BF16, off=RING_OFF + i * 16384) for i in range(4)])
        wchan = [P.dma_chan(f"w{i}") for i in range(4)]
        wchan32 = [P.dma_chan(f"wb{i}") for i in range(2)]

        def ring_mode16():
            for i in range(4):
                ring16.free[i] = list(ring32.free[i // 2])
            ring16.i = 0

        def ring_mode32():
            for i in range(2):
                ring32.free[i] = list(ring16.free[2 * i]) + list(ring16.free[2 * i + 1])
            ring32.i = 0

        ch_small = P.dma_chan("small")
        ch_dbg = P.dma_chan("dbg")
        dbg_toks = []

        def dump(name, ap, tok):
            if name in dbg_out:
                dbg_toks.append(P.dma('sp', ch_dbg, dbg_out[name], ap, wait=[tok]))

        for dst, src in [(cst_f, t["consts"]), (cTt, t["cT"]), (g1c, t["g1col"]), (g2c, t["g2col"]), (rnnc, t["rnn_cols"]),
                         (gqk, t["gqk"]), (ffnc, t["ffn_cols"])]:
            t_small = P.dma('sp', ch_small, dst, src)
        t0 = op('dve', lambda e: e.tensor_copy(out=identb, in_=cst_f[:, 0, :]), wait=[t_small])
        t0 = op('dve', lambda e: e.tensor_copy(out=tgeb, in_=cst_f[:, 1, :]))
        t0 = op('dve', lambda e: e.tensor_copy(out=onesb, in_=cst_f[:, 2, :]))
        t0 = op('dve', lambda e: e.tensor_scalar(out=onesnb, in0=cst_f[:, 2, :], scalar1=-1.0, scalar2=None, op0=ALU.mult))
        t_cst = op('dve', lambda e: e.tensor_copy(out=maskub, in_=cst_f[:, 3, :]))
        t_halo = op('dve', lambda e: e.memset(halo, 0.0))
        t_sc0 = op('act', lambda e: e.activation(out=scol, in_=cTt, func=AF.Silu), wait=[t_small])
        t_SC = op('dve', lambda e: e.tensor_copy(out=SC, in_=scol.unsqueeze(2).to_broadcast([128, 16, 128])), wait=[t_sc0])
        rc = rnnc.rearrange("p (c i) -> p c i", i=8)
        rd = rnnd.rearrange("p (c i) -> p c i", i=4)
        t1 = op('act', lambda e: e.activation(out=rtmp[:, 0:8], in_=rc[:, :, 7], func=AF.Exp, scale=-1.0), wait=[t_small])
        t1 = op('act', lambda e: e.activation(out=rtmp[:, 8:16], in_=rtmp[:, 0:8], func=AF.Ln, bias=1.0, scale=1.0), wait=[t1])
        t2 = op('dve', lambda e: e.tensor_scalar(out=rd[:, :, 0], in0=rtmp[:, 8:16], scalar1=-8.0, scalar2=None, op0=ALU.mult), wait=[t1])
        t2 = op('dve', lambda e: e.tensor_scalar(out=rd[:, :, 1], in0=rtmp[:, 8:16], scalar1=-4.0, scalar2=None, op0=ALU.mult))
        t2 = op('dve', lambda e: e.tensor_scalar(out=rd[:, :, 2], in0=rc[:, :, 5], scalar1=0.5, scalar2=None, op0=ALU.mult))
        t2 = op('dve', lambda e: e.tensor_scalar(out=rd[:, :, 3], in0=rc[:, :, 6], scalar1=0.5, scalar2=None, op0=ALU.mult))
        t_cols = op('dve', lambda e: e.tensor_scalar(out=gqs, in0=gqk[:, 0:1], scalar1=float(128 ** -0.5), scalar2=None, op0=ALU.mult))

        A.off = BIG_OFF
        segbufs = Ring([A.alloc([128, D], F32) for _ in range(2)])
        bsegs = Ring([A.alloc([128, D], F32) for _ in range(2)])
        ch_bseg = [P.dma_chan(f"bseg{i}") for i in range(2)]
        ch_gscr = [P.dma_chan(f"gscr{i}") for i in range(2)]
        ch_colw = [P.dma_chan(f"colw{i}") for i in range(4)]; ch_colr = [P.dma_chan(f"colr{i}") for i in range(4)]
        ada_banks = BankRing([0, 1, 2, 3, 4, 5])
        colbank = 6
        gscr_tok = [None, None]
        colseg = {0: 0, 1: 1, 3: 2, 4: 3}
        t_colcopy = {}
        def make_ada_items(segs, segbufs_, bsegs_, slot_fn):
            blocks = [(seg, blk) for seg in segs for blk in range(2)]
            stt = {}
            wst = {}

            def loadw(k):
                seg, blk = blocks[k]
                nb = seg * 2 + blk
                wst[k] = slot_fn(t["w_ada"][:, nb * 1024:(nb + 1) * 1024].rearrange("(kc p) n -> p kc n", p=128))
            items = []
            for k, (seg, blk) in enumerate(blocks):
                for half in range(2):
                    def unit(k=k, seg=seg, blk=blk, half=half):
                        if blk == 0 and half == 0:
                            si, segbuf, seg_free = segbufs_.acquire()
                            bi, bseg, bseg_free = bsegs_.acquire()
                            t_b = P.dma('sp', ch_bseg[bi], bseg, t["b_ada"][0:1, seg * D:(seg + 1) * D].partition_broadcast(128), wait=bseg_free)
                            stt[seg] = dict(si=si, segbuf=segbuf, seg_free=seg_free, bi=bi, bseg=bseg, t_b=t_b, evs=[])
                        if k == 0 and half == 0:
                            loadw(0)
                        s_ = stt[seg]
                        segbuf, bseg = s_["segbuf"], s_["bseg"]
                        rel, slot, t_w = wst[k]
                        b, bfree = ada_banks.acquire()
                        tok = mm_group(bank(b), [(SC[:, kc, :], slot[:, kc, half * 512:(half + 1) * 512]) for kc in range(NKC)],
                                       waits=[t_w, t_SC] + bfree)
                        c0 = blk * 1024 + half * 512
                        t_e = op('dve', lambda e, b=b, c0=c0, segbuf=segbuf, bseg=bseg: e.tensor_tensor(
                            out=segbuf[:, c0:c0 + 512], in0=bank(b), in1=bseg[:, c0:c0 + 512], op=ALU.add), wait=[tok, s_["t_b"]] + s_["seg_free"])
                        bank_free[b] = [t_e]
                        s_["evs"].append(t_e)
                        if half == 1:
                            rel([tok])
                            if k + 1 < len(blocks):
                                loadw(k + 1)
                        if blk == 1 and half == 1:
                            evs = s_["evs"]
                            bsegs_.release(s_["bi"], [evs[-1]])
                            if seg in colseg:
                                ci = colseg[seg]
                                t_cw = P.dma('sp', ch_colw[ci], col_scr[ci:ci + 1, :], segbuf[0:1, :], wait=evs)
                                mc = modcols[ci]
                                t_cc = P.dma('sp', ch_colr[ci], mc, col_scr[ci:ci + 1, :].rearrange("o (kc p) -> p (o kc)", p=128), wait=[t_cw], slow=True)
                                t_colcopy[seg] = t_cc
                                segbufs_.release(s_["si"], [t_cw])
                            else:
                                gi = 0 if seg == 2 else 1
                                gscr_tok[gi] = P.dma('sp', ch_gscr[gi], gate_scr[gi], segbuf, wait=evs)
                                segbufs_.release(s_["si"], [gscr_tok[gi]])
                    items.append(unit)
            return items

        def slot32(src):
            wi, slot, wfree = ring32.acquire()
            t_w = P.dma('pool', wchan32[wi], slot, src, wait=wfree)
            return (lambda toks, wi=wi: ring32.release(wi, toks)), slot, t_w
        for it in make_ada_items([0, 1], segbufs, bsegs, slot32):
            it()
        t_A1 = op('dve', lambda e: e.scalar_tensor_tensor(out=A1col, in0=modcols[1], scalar=1.0, in1=g1c, op0=ALU.add, op1=ALU.mult),
                  wait=[t_colcopy[1], t_small])
        t_B1 = t_colcopy[0]
        dump("A1col", A1col, t_A1); dump("B1col", B1col, t_B1)

        def finish():
            P.wait_only('sp', dbg_toks[-1:])
            with nc.Block() as block:
                P.run(block)
            return nc

        if stop_after == "adaln":
            return finish()

        small_i = [0]
        xn_state = {}

        def norm_tile(src, src_toks, Acol, Bcol, AB_toks, dst, xnring, pairring):
            k = small_i[0] % 4
            small_i[0] += 1
            ss = small[0][:, k:k + 1]; lnv = small[1][:, k:k + 1]; rstd = small[2][:, k:k + 1]
            xi, xn, xfree = xnring.acquire()
            t_sq = op('act', lambda e: e.activation(out=xn, in_=src, func=AF.Square, accum_out=ss), wait=list(src_toks) + xfree)
            t_ln = op('act', lambda e: e.activation(out=lnv, in_=ss, func=AF.Ln, scale=1.0 / D, bias=EPS), wait=[t_sq])
            t_rs = op('act', lambda e: e.activation(out=rstd, in_=lnv, func=AF.Exp, scale=-0.5), wait=[t_ln])
            t_xn = op('dve', lambda e: e.tensor_scalar(out=xn, in0=src, scalar1=rstd, scalar2=None, op0=ALU.mult), wait=[t_rs] + list(src_toks))
            pi, pb, pfree = pairring.acquire()
            for kc in range(NKC):
                t_tr = op('pe', lambda e, kc=kc: e.transpose(out=pb[:, kc, :], in_=xn[:, kc * 128:(kc + 1) * 128], identity=identb),
                          wait=([t_xn, t_cst] + pfree if kc == 0 else ()), sig=(kc == NKC - 1))
            for kc in range(NKC):
                t_ev = op('act', lambda e, kc=kc: e.activation(out=dst[:, kc, :], in_=pb[:, kc, :], func=AF.Identity,
                                                              scale=Acol[:, kc:kc + 1], bias=Bcol[:, kc:kc + 1]),
                          wait=([t_tr] + list(AB_toks) if kc == 0 else ()), sig=(kc == NKC - 1))
            pairring.release(pi, [t_ev])
            xnring.release(xi, [t_tr])
            return [t_sq, t_xn], t_ev

        def make_pairring(pairs):
            aps = []
            for (b0, b1) in pairs:
                aps.append(ps_t[:, b0:b1 + 1, :].bitcast(BF16).rearrange("p a (k n) -> p (a k) n", n=128))
            r = Ring(aps)
            r.pairs = pairs
            return r

        P.barrier(('act', 'dve', 'sp'), ch_gscr + ch_colw + ch_bseg)
        A.off = BIG_OFF
        hT = A.alloc([128, 16, S], BF16)
        P1_OFF = A.off
        xring = Ring([A.alloc([128, D], F32) for _ in range(2)])
        ch_x = [P.dma_chan(f"x{i}") for i in range(2)]
        xnring = Ring([A.alloc([128, D], BF16) for _ in range(2)])
        pairring = make_pairring([(0, 1), (2, 3)])
        for i, (b0, b1) in enumerate(pairring.pairs):
            pairring.free[i] = list(bank_free[b0]) + list(bank_free[b1])
        t_hT = None
        for tt in range(16):
            xi, xt, xfree = xring.acquire()
            t_x = P.dma('sp', ch_x[xi], xt, t["x"][tt * 128:(tt + 1) * 128, :], wait=xfree)
            rd_toks, t_hT = norm_tile(xt, [t_x], A1col, B1col, [t_A1, t_B1], hT[:, :, tt * 128:(tt + 1) * 128], xnring, pairring)
            xring.release(xi, rd_toks)
        for i, (b0, b1) in enumerate(pairring.pairs):
            bank_free[b0] = list(pairring.free[i]); bank_free[b1] = list(pairring.free[i])
        dump("hT", hT.rearrange("p a b -> p (a b)"), t_hT)
        if stop_after == "norm1":
            return finish()

        P.barrier(('act', 'dve', 'sp'), ch_x)
        A.off = P1_OFF
        gT = A.alloc([128, 8, S], BF16)
        P2_OFF = A.off
        wab = A.alloc([128, 8, 128], BF16)
        wxb = A.alloc([128, 8, 128], BF16)
        ch_wbd = P.dma_chan("wbd")
        P.dma('pool', ch_wbd, wab, t["wa_bd"])
        t_wab = P.dma('pool', ch_wbd, wxb, t["wx_bd"])
        t_wxb = t_wab

        def ring_of(n, shape, dt):
            return Ring([A.alloc(shape, dt) for _ in range(n)])
        XB = ring_of(2, [128, 515], F32)
        r_xr = ring_of(2, [128, 512], F32); r_xrb = ring_of(2, [128, 512], BF16)
        r_thr = ring_of(2, [128, 512], F32); r_a = ring_of(2, [128, 512], F32); r_a2 = ring_of(2, [128, 512], F32)
        r_thi = ring_of(2, [128, 512], F32); r_hr = ring_of(2, [128, 512], F32); r_ge = ring_of(2, [128, 512], BF16)
        ring_mode16()
        rnn_w = []
        for (c0_, si_) in ((0, 0), (1024, 1), (512, 2), (1536, 3)):
            wi, slot, wfree = ring16.acquire()
            tw = P.dma('pool', wchan[wi], slot, t["w_in"][:, c0_:c0_ + 512].rearrange("(kc p) n -> p kc n", p=128), wait=wfree)
            rnn_w.append((wi, slot, tw))
        rb = BankRing([0, 1, 2, 3, 4, 5, 6, 7])
        last_pe_w = [None] * 4
        prev_xb = None
        prev_hr = None
        t_gT = None
        for c in range(8):
            wiX, Xs, tXw = rnn_w[0 if c < 4 else 2]
            wiG, Gs, tGw = rnn_w[1 if c < 4 else 3]
            cc = c % 4
            colw = lambda j, c=c: rc[:, c, j:j + 1]
            for tb in range(4):
                tsl = slice(tb * 512, (tb + 1) * 512)
                bX, fX = rb.acquire()
                tX = mm_group(bank(bX), [(Xs[:, kc, cc * 128:(cc + 1) * 128], hT[:, kc, tsl]) for kc in range(NKC)], waits=[tXw, t_hT] + fX)
                bG, fG = rb.acquire()
                tG = mm_group(bank(bG), [(Gs[:, kc, cc * 128:(cc + 1) * 128], hT[:, kc, tsl]) for kc in range(NKC)], waits=[tGw] + fG)
                last_pe_w[0 if c < 4 else 2] = tX; last_pe_w[1 if c < 4 else 3] = tG
                xbi, xb, xbfree = XB.acquire()
                if tb == 0:
                    t_h = op('dve', lambda e, xb=xb: e.memset(xb[:, 0:3], 0.0), wait=xbfree)
                else:
                    t_h = op('dve', lambda e, xb=xb, pxb=prev_xb[0]: e.tensor_copy(out=xb[:, 0:3], in_=pxb[:, 512:515]), wait=xbfree + [prev_xb[1]])
                t_xc = op('act', lambda e, xb=xb, bX=bX: e.activation(out=xb[:, 3:515], in_=bank(bX), func=AF.Copy), wait=[tX] + xbfree)
                bank_free[bX] = [t_xc]
                prev_xb = (xb, t_xc)
                i1, xr, f1 = r_xr.acquire()
                t_c = op('dve', lambda e, xr=xr, xb=xb, c=c: e.tensor_scalar(out=xr, in0=xb[:, 3:515], scalar1=rc[:, c, 3:4], scalar2=rc[:, c, 4:5],
                                                                         op0=ALU.mult, op1=ALU.add), wait=[t_xc, t_h, t_small] + f1)
                for j in (2, 1, 0):
                    t_c = op('dve', lambda e, xr=xr, xb=xb, c=c, j=j: e.scalar_tensor_tensor(out=xr, in0=xb[:, j:j + 512], scalar=rc[:, c, j:j + 1], in1=xr,
                                                                                          op0=ALU.mult, op1=ALU.add), wait=[t_c])
                XB.release(xbi, [t_c])
                i2, xrb, f2 = r_xrb.acquire()
                t_xrb = op('act', lambda e, xrb=xrb, xr=xr: e.activation(out=xrb, in_=xr, func=AF.Copy), wait=[t_c] + f2)
                bR, fR = rb.acquire()
                tR = op('pe', lambda e, bR=bR, c=c, xrb=xrb: e.matmul(bank(bR), lhsT=wab[:, c, :], rhs=xrb, start=True, stop=True), wait=[t_xrb, t_wab] + fR)
                bI, fI = rb.acquire()
                tI = op('pe', lambda e, bI=bI, c=c, xrb=xrb: e.matmul(bank(bI), lhsT=wxb[:, c, :], rhs=xrb, start=True, stop=True), wait=[t_wxb] + fI)
                r_xrb.release(i2, [tI])
                i3, thr, f3 = r_thr.acquire()
                t_thr = op('act', lambda e, thr=thr, bR=bR, c=c: e.activation(out=thr, in_=bank(bR), func=AF.Tanh, scale=0.5, bias=rd[:, c, 2:3]), wait=[tR, t_cols] + f3)
                bank_free[bR] = [t_thr]
                i4, av, f4 = r_a.acquire()
                t_a = op('act', lambda e, av=av, thr=thr, c=c: e.activation(out=av, in_=thr, func=AF.Exp, scale=rd[:, c, 1:2], bias=rd[:, c, 1:2]), wait=[t_thr] + f4)
                i5, a2, f5 = r_a2.acquire()
                t_a2 = op('act', lambda e, a2=a2, thr=thr, c=c: e.activation(out=a2, in_=thr, func=AF.Exp, scale=rd[:, c, 0:1], bias=rd[:, c, 0:1]), wait=[t_thr] + f5)
                r_thr.release(i3, [t_a2])
                i6, thi, f6 = r_thi.acquire()
                t_thi = op('act', lambda e, thi=thi, bI=bI, c=c: e.activation(out=thi, in_=bank(bI), func=AF.Tanh, scale=0.5, bias=rd[:, c, 3:4]), wait=[tI] + f6)
                bank_free[bI] = [t_thi]
                t_u = op('dve', lambda e, a2=a2: e.tensor_scalar(out=a2, in0=a2, scalar1=1.0, scalar2=-1.0, op0=ALU.min, op1=ALU.mult), wait=[t_a2])
                t_s = op('act', lambda e, a2=a2: e.activation(out=a2, in_=a2, func=AF.Sqrt, bias=1.0, scale=1.0), wait=[t_u])
                t_b = op('dve', lambda e, thi=thi, xr=xr: e.scalar_tensor_tensor(out=thi, in0=thi, scalar=1.0, in1=xr, op0=ALU.add, op1=ALU.mult), wait=[t_thi, t_c])
                r_xr.release(i1, [t_b, t_xrb])
                t_b = op('dve', lambda e, thi=thi, a2=a2: e.scalar_tensor_tensor(out=thi, in0=thi, scalar=0.5, in1=a2, op0=ALU.mult, op1=ALU.mult), wait=[t_b, t_s])
                r_a2.release(i5, [t_b])
                i7, hr, f7 = r_hr.acquire()
                if tb == 0:
                    t_hr = op('dve', lambda e, hr=hr, av=av, thi=thi: e.tensor_tensor_scan(out=hr, data0=av, data1=thi, initial=0.0, op0=ALU.mult, op1=ALU.add),
                              wait=[t_a, t_b] + f7)
                else:
                    t_hr = op('dve', lambda e, hr=hr, av=av, thi=thi, ph=prev_hr[0]: e.tensor_tensor_scan(out=hr, data0=av, data1=thi, initial=ph[:, 511:512],
                                                                                                         op0=ALU.mult, op1=ALU.add),
                              wait=[t_a, t_b, prev_hr[1]] + f7)
                r_a.release(i4, [t_hr]); r_thi.release(i6, [t_hr])
                i8, ge, f8 = r_ge.acquire()
                t_ge = op('act', lambda e, ge=ge, bG=bG: e.activation(out=ge, in_=bank(bG), func=AF.Gelu_apprx_tanh), wait=[tG] + f8)
                bank_free[bG] = [t_ge]
                t_gT = op('dve', lambda e, ge=ge, hr=hr, c=c, tsl=tsl: e.tensor_tensor(out=gT[:, c, tsl], in0=ge, in1=hr, op=ALU.mult), wait=[t_ge, t_hr])
                r_ge.release(i8, [t_gT])
                if prev_hr is not None:
                    r_hr.release(prev_hr[2], [t_hr, prev_hr[3]])
                prev_hr = (hr, t_hr, i7, t_gT)
            r_hr.release(prev_hr[2], [prev_hr[3]])
            prev_hr = None
            prev_xb = None
        for k_, (wi, slot, tw) in enumerate(rnn_w):
            ring16.release(wi, [last_pe_w[k_]])
        dump("gT", gT.rearrange("p a b -> p (a b)"), t_gT)
        if stop_after == "rnn":
            return finish()

        P.barrier(('act', 'dve', 'sp', 'pool'), wchan + wchan32 + [ch_wbd])
        A.off = P2_OFF
        oT = A.alloc([128, 8, S], BF16)
        P3_OFF = A.off
        stg = Ring([A.alloc([128, 512], F32) for _ in range(2)])
        ch_stg = [P.dma_chan(f"stg{i}") for i in range(2)]
        AR = Arena(arena_t, ARENA_BYTES)
        AR.off = RING_OFF

        def ring_r(n, shape, dt):
            r = Ring([AR.alloc(shape, dt) for _ in range(n)])
            assert AR.off <= RING_OFF + 49152, "ring region overflow"
            return r
        Wq = ring_r(1, [128, 16, 128], BF16); Wk = ring_r(1, [128, 16, 128], BF16); Wv = ring_r(1, [128, 16, 128], BF16)
        ch_hw = [[P.dma_chan(f"hw{j}{i}") for i in range(1)] for j in range(3)]
        qT = AR.alloc([128, S], BF16); kT = AR.alloc([128, S], BF16); vh = AR.alloc([128, 16, 128], BF16)
        r_sq = ring_r(2, [128, 512], BF16); r_ln = ring_r(2, [128, 512], F32)
        r_E = ring_r(2, [128, 512], F32); r_SP = ring_r(2, [128, 512], BF16); r_X = ring_r(2, [128, 512], F32)
        r_W = ring_r(2, [128, 512], BF16); r_sum = ring_r(2, [128, 512], BF16)
        r_g1 = ring_r(1, [128, 512], F32)
        gslot = ring16.aps[3]
        pj = BankRing([0, 1]); ssb = BankRing([2, 3])
        zbr = BankRing([0, 1]); bbr = BankRing([2, 3]); otr = BankRing([4, 5]); gbr = BankRing([6, 7])
        qkv_free = {"q": [], "k": [], "v": []}
        t_oT = None
        sg_tok = {}
        import collections
        gate_items = collections.deque()

        def make_gate_block(blk):
            state = {}

            def load():
                state["tw"] = P.dma('pool', wchan[3], gslot, t["w_in"][:, 5120 + blk * 512: 5120 + (blk + 1) * 512].rearrange("(kc p) n -> p kc n", p=128),
                                    wait=list(state.get("free", [])))
            items = [load]
            for oci in range(4):
                for tb in range(4):
                    gst = {}

                    def half1(oci=oci, tb=tb, gst=gst):
                        tsl = slice(tb * 512, (tb + 1) * 512)
                        b, bf = gbr.acquire()
                        gst["b"] = b
                        for kc in range(8):
                            op('pe', lambda e, b=b, kc=kc, oci=oci, tsl=tsl: e.matmul(bank(b), lhsT=gslot[:, kc, oci * 128:(oci + 1) * 128], rhs=hT[:, kc, tsl],
                                                                                   start=(kc == 0), stop=False),
                               wait=([state["tw"]] + bf if kc == 0 else ()), sig=False)

                    def half2(oci=oci, tb=tb, gst=gst):
                        oc = blk * 4 + oci
                        tsl = slice(tb * 512, (tb + 1) * 512)
                        b = gst["b"]
                        for kc in range(8, 16):
                            tp = op('pe', lambda e, b=b, kc=kc, oci=oci, tsl=tsl: e.matmul(bank(b), lhsT=gslot[:, kc, oci * 128:(oci + 1) * 128], rhs=hT[:, kc, tsl],
                                                                                        start=False, stop=(kc == 15)), sig=(kc == 15))
                        i1, g1, f1 = r_g1.acquire()
                        t_e = op('act', lambda e, g1=g1, b=b: e.activation(out=g1, in_=bank(b), func=AF.Exp, scale=-1.0), wait=[tp] + f1)
                        bank_free[b] = [t_e]
                        t_p = op('act', lambda e, g1=g1: e.activation(out=g1, in_=g1, func=AF.Ln, bias=1.0, scale=1.0), wait=[t_e])
                        si, sb_, sfree = stg.acquire()
                        t_s = op('act', lambda e, sb_=sb_, g1=g1: e.activation(out=sb_, in_=g1, func=AF.Exp, scale=-1.0), wait=[t_p] + sfree)
                        r_g1.release(i1, [t_s])
                        t_st = P.dma('sp', ch_stg[si], sg[oc][:, tsl], sb_, wait=[t_s])
                        stg.release(si, [t_st])
                        sg_tok[(oc, tb)] = t_st
                        state["last"] = tp
                    items.append(half1)
                    items.append(half2)
            return items, state
        gate_states = []
        for blk in range(8):
            items, stt_ = make_gate_block(blk)
            gate_states.append(stt_)
            gate_items.append(items)

        def load_head_w(h):
            res = []
            for j, (Wr, base) in enumerate(((Wq, 2048), (Wk, 3072), (Wv, 4096))):
                wi, w, wfree = Wr.acquire()
                tw = P.dma('pool', ch_hw[j][wi], w, t["w_in"][:, base + h * 128: base + (h + 1) * 128].rearrange("(kc p) n -> p kc n", p=128), wait=wfree)
                res.append((wi, w, tw))
            return res
        hw_next = load_head_w(0)
        for h in range(8):
            hw = hw_next
            my_gate = list(gate_items.popleft())
            if h > 0:
                gate_states[h]["free"] = [gate_states[h - 1]["last"]]
            my_gate.pop(0)()
            for j, (dstT, gcol, key) in enumerate(((qT, gqs[:, 0:1], "q"), (kT, gqk[:, 1:2], "k"))):
                wi, w, tw = hw[j]
                for tb in range(4):
                    tsl = slice(tb * 512, (tb + 1) * 512)
                    b, bf = pj.acquire()
                    tp = mm_group(bank(b), [(w[:, kc, :], hT[:, kc, tsl]) for kc in range(NKC)], waits=[tw] + bf)
                    last_w = tp
                    i1, sq, f1 = r_sq.acquire()
                    t_sq = op('act', lambda e, sq=sq, b=b: e.activation(out=sq, in_=bank(b), func=AF.Square), wait=[tp] + f1)
                    b2, bf2 = ssb.acquire()
                    t_ss = op('pe', lambda e, b2=b2, sq=sq: e.matmul(bank(b2), lhsT=onesb, rhs=sq, start=True, stop=True), wait=[t_sq, t_cst] + bf2)
                    r_sq.release(i1, [t_ss])
                    i2, ln, f2 = r_ln.acquire()
                    t_ln = op('act', lambda e, ln=ln, b2=b2: e.activation(out=ln, in_=bank(b2), func=AF.Ln, scale=1.0 / 128, bias=EPS), wait=[t_ss] + f2)
                    bank_free[b2] = [t_ln]
                    t_R = op('act', lambda e, ln=ln: e.activation(out=ln, in_=ln, func=AF.Exp, scale=-0.5), wait=[t_ln])
                    t_q = op('dve', lambda e, dstT=dstT, b=b, gcol=gcol, ln=ln, tsl=tsl: e.scalar_tensor_tensor(out=dstT[:, tsl], in0=bank(b), scalar=gcol, in1=ln,
                                                                                                        op0=ALU.mult, op1=ALU.mult),
                              wait=[tp, t_R, t_cols] + qkv_free[key])
                    bank_free[b] = [t_q]
                    r_ln.release(i2, [t_q])
                (Wq if j == 0 else Wk).release(wi, [last_w])
                if j == 0:
                    t_qT = t_q
                else:
                    t_kT = t_q
            wi, w, tw = hw[2]
            for tg in range(4):
                b, bf = pj.acquire()
                for tti in range(4):
                    tt = tg * 4 + tti
                    tp = mm_group(bank(b)[:, tti * 128:(tti + 1) * 128], [(hT[:, kc, tt * 128:(tt + 1) * 128], w[:, kc, :]) for kc in range(NKC)],
                                  waits=([tw] + bf if tti == 0 else ()), sig=(tti == 3))
                t_v = op('dve', lambda e, b=b, tg=tg: e.tensor_copy(out=vh[:, tg * 4:(tg + 1) * 4, :].rearrange("p a b -> p (a b)"), in_=bank(b)),
                         wait=[tp] + qkv_free["v"])
                bank_free[b] = [t_v]
            Wv.release(wi, [tp])
            if h + 1 < 8:
                hw_next = load_head_w(h + 1)
            units = [(qc, sb) for qc in range(4) for sb in range(4 * qc + 3, -1, -1)]
            st = {}
            last_read = {"q": None, "k": None, "v": None}

            def unit_a1(qc, sb):
                c0 = max(0, 128 * (sb - 4 * qc)); N = 512 - c0; qa = 512 * qc + c0; diag = sb >= 4 * qc
                first = (sb == 4 * qc + 3)
                if first:
                    ci, ssum, cfree = r_sum.acquire()
                    t_cm = op('dve', lambda e, ssum=ssum: e.memset(ssum, 0.0), wait=cfree)
                    bo, bof = otr.acquire()
                    st[qc] = {"sum": ssum, "ci": ci, "t_sum": t_cm, "ot": bo, "otfree": bof}
                ksl = slice(sb * 128, (sb + 1) * 128); qsl = slice(qa, qa + N)
                bz, fz = zbr.acquire()
                t_z = op('pe', lambda e, bz=bz, ksl=ksl, qsl=qsl, N=N: e.matmul(bank(bz)[:, 0:N], lhsT=kT[:, ksl], rhs=qT[:, qsl], start=True, stop=True),
                         wait=[t_qT, t_kT] + fz)
                i1, E, f1 = r_E.acquire()
                t_E = op('act', lambda e, E=E, bz=bz, N=N: e.activation(out=E[:, 0:N], in_=bank(bz)[:, 0:N], func=AF.Exp), wait=[t_z] + f1)
                bank_free[bz] = [t_E]
                i2, SP, f2 = r_SP.acquire()
                t_SP = op('act', lambda e, SP=SP, E=E, N=N: e.activation(out=SP[:, 0:N], in_=E[:, 0:N], func=AF.Ln, bias=1.0, scale=1.0), wait=[t_E] + f2)
                if diag:
                    t_SP = op('dve', lambda e, SP=SP: e.tensor_tensor(out=SP[:, 0:128], in0=SP[:, 0:128], in1=maskub, op=ALU.mult), wait=[t_SP, t_cst])
                last_read["q"] = t_z; last_read["k"] = t_z
                return dict(qc=qc, sb=sb, c0=c0, N=N, diag=diag, first=first, E=E, iE=i1, SP=SP, iSP=i2, t_SP=t_SP, t_E=t_E)

            def unit_a2(u):
                qc, sb, c0, N, SP = u["qc"], u["sb"], u["c0"], u["N"], u["SP"]
                s_ = st[qc]; ssum = s_["sum"]
                bb_, fb = bbr.acquire()
                first = u["first"]
                t_B = op('pe', lambda e, bb_=bb_, SP=SP, N=N, first=first: e.matmul(bank(bb_)[:, 0:N], lhsT=tgeb, rhs=SP[:, 0:N], start=True, stop=first),
                         wait=[u["t_SP"]] + fb)
                if not first:
                    t_B = op('pe', lambda e, bb_=bb_, ssum=ssum, N=N, c0=c0: e.matmul(bank(bb_)[:, 0:N], lhsT=onesnb, rhs=ssum[:, c0:512], start=False, stop=True),
                             wait=[s_["t_sum"]])
                t_su = op('dve', lambda e, ssum=ssum, SP=SP, N=N, c0=c0: e.tensor_tensor(out=ssum[:, c0:512], in0=ssum[:, c0:512], in1=SP[:, 0:N], op=ALU.add),
                          wait=[t_B, u["t_SP"], s_["t_sum"]])
                s_["t_sum"] = t_su
                r_SP.release(u["iSP"], [t_su, t_B])
                u["bb"] = bb_; u["t_B"] = t_B

            def unit_b1(u):
                N, c0, E = u["N"], u["c0"], u["E"]
                bb_ = u["bb"]
                i0, X, f0 = r_X.acquire()
                t_X = op('act', lambda e, X=X, bb_=bb_, N=N: e.activation(out=X[:, 0:N], in_=bank(bb_)[:, 0:N], func=AF.Exp), wait=[u["t_B"]] + f0)
                bank_free[bb_] = [t_X]
                i1, W, f1 = r_W.acquire()
                first = u["first"]
                wo = c0 if first else 0
                t_z0 = None
                if first:
                    t_z0 = op('dve', lambda e, W=W, c0=c0: e.memset(W[:, 0:c0], 0.0), wait=f1)
                t_W = op('dve', lambda e, W=W, X=X, E=E, N=N, wo=wo: e.tensor_tensor(out=W[:, wo:wo + N], in0=E[:, 0:N], in1=X[:, 0:N], op=ALU.mult),
                         wait=[t_X, u["t_E"]] + f1)
                r_X.release(i0, [t_W]); r_E.release(u["iE"], [t_W])
                if u["diag"]:
                    t_W = op('dve', lambda e, W=W, wo=wo: e.tensor_tensor(out=W[:, wo:wo + 128], in0=W[:, wo:wo + 128], in1=maskub, op=ALU.mult), wait=[t_W])
                u["W"] = W; u["iW"] = i1; u["t_W"] = t_W; u["t_z0"] = t_z0

            def unit_b2(u):
                nonlocal t_oT
                qc, sb, c0, N, W = u["qc"], u["sb"], u["c0"], u["N"], u["W"]
                s_ = st[qc]
                bo = s_["ot"]
                if u["first"]:
                    t_O = op('pe', lambda e, bo=bo, W=W, sb=sb: e.matmul(bank(bo), lhsT=vh[:, sb, :], rhs=W, start=True, stop=(sb == 0)),
                             wait=[u["t_W"], u["t_z0"], t_v] + s_["otfree"])
                else:
                    t_O = op('pe', lambda e, bo=bo, W=W, sb=sb, N=N, c0=c0: e.matmul(bank(bo)[:, c0:512], lhsT=vh[:, sb, :], rhs=W[:, 0:N],
                                                                                    start=False, stop=(sb == 0)), wait=[u["t_W"], t_v])
                r_W.release(u["iW"], [t_O])
                last_read["v"] = t_O
                if sb == 0:
                    t_oT = op('act', lambda e, bo=bo, h=h, qc=qc: e.activation(out=oT[:, h, qc * 512:(qc + 1) * 512], in_=bank(bo), func=AF.Copy), wait=[t_O])
                    bank_free[bo] = [t_oT]
                    r_sum.release(s_["ci"], [s_["t_sum"]])

            prev = None
            for ui, (qc, sb) in enumerate(units):
                if prev is not None:
                    unit_b1(prev)
                u = unit_a1(qc, sb)
                if my_gate:
                    my_gate.pop(0)()
                if prev is not None:
                    unit_b2(prev)
                unit_a2(u)
                prev = u
            unit_b1(prev); unit_b2(prev)
            while my_gate:
                my_gate.pop(0)()
            qkv_free["q"] = [last_read["q"]]; qkv_free["k"] = [last_read["k"]]; qkv_free["v"] = [last_read["v"]]
        dump("oT", oT.rearrange("p a b -> p (a b)"), t_oT)
        if stop_after == "attn":
            return finish()
        gb = BankRing([0, 1, 2, 3, 4, 5, 6, 7])
        ring16.free = [[] for _ in range(4)]
        ring16.free[3] = [gate_states[7]["last"]]
        ring16.i = 0

        P.barrier(('act', 'dve', 'sp', 'pool'), ch_stg + [c_ for row in ch_hw for c_ in row] + wchan)
        A.off = BIG_OFF
        sgr_r = Ring([A.alloc([128, S], F32) for _ in range(2)]); sgs_r = Ring([A.alloc([128, S], F32) for _ in range(2)])
        ch_sgr = [P.dma_chan(f"sgr{i}") for i in range(2)]; ch_sgs = [P.dma_chan(f"sgs{i}") for i in range(2)]
        r_t1 = Ring([A.alloc([128, 512], F32) for _ in range(2)]); r_t2 = Ring([A.alloc([128, 512], F32) for _ in range(2)])
        mst = Ring([A.alloc([128, 512], BF16) for _ in range(4)])
        ch_mst = [P.dma_chan(f"mst{i}") for i in range(4)]
        assert A.off <= BIG_OFF + 65536
        mT_tok = {}
        segbufs2 = Ring([A.alloc([128, D], F32)]); bsegs2 = Ring([A.alloc([128, D], F32)])
        assert A.off <= BIG_OFF + 65536
        aslot = ring32.aps[1]
        aslot_state = {"free": list(ring16.free[2]) + list(ring16.free[3])}

        def slot_single(src):
            t_w = P.dma('pool', wchan32[1], aslot, src, wait=aslot_state["free"])

            def rel(toks):
                aslot_state["free"] = list(toks)
            return rel, aslot, t_w
        ada_banks = gb
        ada2 = make_ada_items([2, 3, 4, 5], segbufs2, bsegs2, slot_single)
        ring16m = Ring(ring16.aps[0:2])
        ring16m.free = [list(ring16.free[0]), list(ring16.free[1])]

        def load_pj(blk):
            wi, slot, wfree = ring16m.acquire()
            P.dma('pool', wchan[wi], slot[:, 0:8, :], t["w_proj_rnn"][:, blk * 512:(blk + 1) * 512].rearrange("(kc p) n -> p kc n", p=128), wait=wfree)
            tw = P.dma('pool', wchan[wi], slot[:, 8:16, :], t["w_proj_sb"][:, blk * 512:(blk + 1) * 512].rearrange("(kc p) n -> p kc n", p=128))
            return wi, slot, tw
        pj_loaded = {0: load_pj(0)}
        nunit = 0
        for blk in range(4):
            wi, slot, tw = pj_loaded[blk]
            if blk + 1 < 4 and blk >= 1:
                pass
            for oci in range(4):
                oc = blk * 4 + oci
                ri, sgr_t, rfree = sgr_r.acquire()
                t_lr = P.dma('sp', ch_sgr[ri], sgr_t, sg[oc], wait=rfree + [sg_tok[(oc, 3)]])
                si2, sgs_t, sfree2 = sgs_r.acquire()
                t_ls = P.dma('sp', ch_sgs[si2], sgs_t, sg[16 + oc], wait=sfree2 + [sg_tok[(16 + oc, 3)]])
                for tb in range(4):
                    tsl = slice(tb * 512, (tb + 1) * 512)
                    b1, bf1 = gb.acquire()
                    tp1 = mm_group(bank(b1), [(slot[:, kc, oci * 128:(oci + 1) * 128], gT[:, kc, tsl]) for kc in range(8)], waits=[tw, t_gT] + bf1)
                    b2, bf2 = gb.acquire()
                    tp2 = mm_group(bank(b2), [(slot[:, 8 + kc, oci * 128:(oci + 1) * 128], oT[:, kc, tsl]) for kc in range(8)], waits=[t_oT] + bf2)
                    i1, t1_, f1 = r_t1.acquire()
                    t_a1 = op('dve', lambda e, t1_=t1_, b1=b1, sgr_t=sgr_t, tsl=tsl: e.tensor_tensor(out=t1_, in0=bank(b1), in1=sgr_t[:, tsl], op=ALU.mult),
                              wait=[tp1, t_lr] + f1)
                    bank_free[b1] = [t_a1]
                    i2, t2_, f2 = r_t2.acquire()
                    t_a2 = op('dve', lambda e, t2_=t2_, b2=b2, sgs_t=sgs_t, tsl=tsl: e.tensor_tensor(out=t2_, in0=bank(b2), in1=sgs_t[:, tsl], op=ALU.mult),
                              wait=[tp2, t_ls] + f2)
                    bank_free[b2] = [t_a2]
                    mi, m_, mfree = mst.acquire()
                    t_m = op('dve', lambda e, m_=m_, t1_=t1_, t2_=t2_: e.tensor_tensor(out=m_, in0=t1_, in1=t2_, op=ALU.add), wait=[t_a1, t_a2] + mfree)
                    r_t1.release(i1, [t_m]); r_t2.release(i2, [t_m])
                    t_ms = P.dma('sp', ch_mst[mi], mT[oc][:, tsl], m_, wait=[t_m])
                    mst.release(mi, [t_ms])
                    mT_tok[(oc, tb)] = t_ms
                    nunit += 1
                    if nunit % 16 == 2 and blk + 1 < 4:
                        pj_loaded[blk + 1] = load_pj(blk + 1)
                    if nunit % 4 == 0 and ada2:
                        ada2.pop(0)()
                sgr_r.release(ri, [t_a1]); sgs_r.release(si2, [t_a2])
            ring16m.release(wi, [tp2])
            nunit = 0 if False else nunit
            if blk + 1 < 4 and (blk + 1) not in pj_loaded:
                pj_loaded[blk + 1] = load_pj(blk + 1)
        while ada2:
            ada2.pop(0)()
        t_A2 = op('dve', lambda e: e.scalar_tensor_tensor(out=A2col, in0=modcols[3], scalar=1.0, in1=g2c, op0=ALU.add, op1=ALU.mult),
                  wait=[t_colcopy[4], t_small])
        t_B2 = t_colcopy[3]
        ring16.free[0] = list(ring16m.free[0]); ring16.free[1] = list(ring16m.free[1])
        ring16.free[2] = list(aslot_state["free"]); ring16.free[3] = list(aslot_state["free"])
        if stop_after == "merge":
            P.wait_only('sp', [mT_tok[(15, tb)] for tb in range(4)])
            return finish()

        P.barrier(('act', 'dve', 'sp', 'pool'), ch_mst + ch_sgr + ch_sgs + wchan + wchan32 + ch_bseg + ch_gscr + ch_colw)
        ring_mode32()
        A.off = BIG_OFF
        acc = A.alloc([128, 4, D], F32)
        h2T = A.alloc([128, 16, 512], BF16)
        actT = A.alloc([128, 16, 512], BF16)
        mtile = A.alloc([128, 16, 512], BF16)
        valbuf = mtile.bitcast(F32) if False else None
        VAL_OFF = A.off - 16384
        valbuf = A.alloc([128, 8, 512], F32, off=VAL_OFF)
        g1bc = A.alloc([128, D], F32); g2bc = A.alloc([128, D], F32)
        xn2 = Ring([A.alloc([128, D], BF16) for _ in range(2)])
        r_tmp = Ring([A.alloc([128, 512], F32) for _ in range(2)])
        r_G = Ring([A.alloc([128, 514], F32) for _ in range(2)])
        r_ca = Ring([A.alloc([128, 512], F32) for _ in range(2)])
        r_ge2 = Ring([A.alloc([128, 512], F32) for _ in range(2)])
        ch_g = P.dma_chan("gbc"); ch_mt = P.dma_chan("mtile")
        ch_acc = [P.dma_chan(f"accx{i}") for i in range(4)]; ch_out = [P.dma_chan(f"outst{i}") for i in range(4)]
        P.dma('sp', ch_g, g1bc, gate_scr[0], wait=[gscr_tok[0]])
        t_gbc = P.dma('sp', ch_g, g2bc, gate_scr[1], wait=[gscr_tok[1]])
        fc = ffnc.rearrange("p (c i) -> p c i", i=4)
        hl = halo.rearrange("p (c i) -> p c i", i=2)
        pb2 = BankRing([0, 1, 2, 3, 4, 5])
        pair2 = make_pairring([(6, 7)])
        pair2.free[0] = list(bank_free[6]) + list(bank_free[7])
        S6 = {"valbuf_free": [], "actT_ready": None, "t_act": None, "t_vals": None, "t_h2": None}
        mT_all = [mT_tok[(15, 3)], mT_tok[(15, 2)], mT_tok[(15, 1)], mT_tok[(15, 0)]]
        t_mt = P.dma('sp', ch_mt, mtile, mT[:, :, 0:512].rearrange("k p n -> p k n"), wait=mT_all)
        t_ax = [P.dma('sp', ch_acc[tt], acc[:, tt, :], t["x"][tt * 128:(tt + 1) * 128, :]) for tt in range(4)]
        out_toks = [None] * 4

        def w_load32(src):
            wi, slot, wfree = ring32.acquire()
            tw = P.dma('pool', wchan32[wi], slot, src.rearrange("(kc p) n -> p kc n", p=128), wait=wfree)
            return wi, slot, tw

        def do_val(j, vb):
            wi, slot, tw = w_load32(t["w_up"][:, j * 2048 + vb * 1024: j * 2048 + (vb + 1) * 1024])
            t_vals = []
            for ci in range(8):
                b, bf = pb2.acquire()
                tp = mm_group(bank(b), [(slot[:, kc, ci * 128:(ci + 1) * 128], h2T[:, kc, :]) for kc in range(NKC)], waits=[tw, S6["t_h2"]] + bf)
                t_vc = op('act', lambda e, b=b, ci=ci: e.activation(out=valbuf[:, ci, :], in_=bank(b), func=AF.Copy),
                          wait=[tp] + S6["valbuf_free"] + [S6["last_mt_read"]])
                bank_free[b] = [t_vc]
                t_vals.append(t_vc)
            ring32.release(wi, [tp])
            S6["t_vals"] = t_vals

        def do_gate(j, vb):
            wi, slot, tw = w_load32(t["w_up"][:, DFF + j * 2048 + vb * 1024: DFF + j * 2048 + (vb + 1) * 1024])
            t_vals = S6["t_vals"]
            for ci in range(8):
                c = vb * 8 + ci
                gc = j * 16 + c
                b, bf = pb2.acquire()
                tp = mm_group(bank(b), [(slot[:, kc, ci * 128:(ci + 1) * 128], h2T[:, kc, :]) for kc in range(NKC)], waits=[tw, S6["t_h2"]] + bf)
                gi, G, gfree = r_G.acquire()
                t_hl = op('dve', lambda e, G=G, gc=gc: e.tensor_copy(out=G[:, 0:2], in_=hl[:, gc, :]), wait=gfree + [t_halo])
                t_gc = op('act', lambda e, G=G, b=b: e.activation(out=G[:, 2:514], in_=bank(b), func=AF.Copy), wait=[tp] + gfree)
                bank_free[b] = [t_gc]
                t_hs = op('dve', lambda e, G=G, gc=gc: e.tensor_copy(out=hl[:, gc, :], in_=G[:, 512:514]), wait=[t_gc, t_hl])
                i2, ca, f2 = r_ca.acquire()
                t_cv = op('dve', lambda e, ca=ca, G=G, gc=gc: e.tensor_scalar(out=ca, in0=G[:, 0:512], scalar1=fc[:, gc, 0:1], scalar2=None, op0=ALU.mult),
                          wait=[t_hl, t_gc] + f2)
                for jj in (1, 2):
                    t_cv = op('dve', lambda e, ca=ca, G=G, gc=gc, jj=jj: e.scalar_tensor_tensor(out=ca, in0=G[:, jj:jj + 512], scalar=fc[:, gc, jj:jj + 1], in1=ca,
                                                                                          op0=ALU.mult, op1=ALU.add), wait=[t_cv])
                r_G.release(gi, [t_cv, t_hs])
                i3, ge, f3 = r_ge2.acquire()
                t_ge = op('act', lambda e, ge=ge, ca=ca, gc=gc: e.activation(out=ge, in_=ca, func=AF.Gelu_apprx_tanh, bias=fc[:, gc, 3:4], scale=1.0),
                          wait=[t_cv] + f3)
                r_ca.release(i2, [t_ge])
                t_act = op('dve', lambda e, ge=ge, c=c, ci=ci: e.tensor_tensor(out=actT[:, c, :], in0=ge, in1=valbuf[:, ci, :], op=ALU.mult),
                           wait=[t_ge, t_vals[ci]] + ([S6["actT_ready"]] if S6["actT_ready"] is not None else []))
                r_ge2.release(i3, [t_act])
            S6["valbuf_free"] = [t_act]
            S6["t_act"] = t_act
            ring32.release(wi, [tp])

        def do_wdown(j, ti):
            final = (j == 2)
            for blk in range(2):
                wi, slot, tw = w_load32(t["w_down"][j * 2048:(j + 1) * 2048, blk * 1024:(blk + 1) * 1024])
                for tt in range(4):
                    for half in range(2):
                        b, bf = pb2.acquire()
                        tp = mm_group(bank(b), [(actT[:, kc, tt * 128:(tt + 1) * 128], slot[:, kc, half * 512:(half + 1) * 512]) for kc in range(NKC)],
                                      waits=[tw, S6["t_act"]] + bf)
                        c0 = blk * 1024 + half * 512
                        i1, tmp, f1 = r_tmp.acquire()
                        t_e1 = op('dve', lambda e, tmp=tmp, b=b, c0=c0: e.tensor_tensor(out=tmp, in0=bank(b), in1=g2bc[:, c0:c0 + 512], op=ALU.mult),
                                  wait=[tp] + f1)
                        bank_free[b] = [t_e1]
                        t_e2 = op('dve', lambda e, tmp=tmp, tt=tt, c0=c0: e.tensor_tensor(out=acc[:, tt, c0:c0 + 512], in0=acc[:, tt, c0:c0 + 512], in1=tmp, op=ALU.add),
                                  wait=[t_e1])
                        r_tmp.release(i1, [t_e2])
                    if final and blk == 1:
                        r0 = ti * 512 + tt * 128
                        out_toks[tt] = P.dma('sp', ch_out[tt], out[r0:r0 + 128, :], acc[:, tt, :], wait=[t_e2])
                        if ti < 3:
                            t_ax[tt] = P.dma('sp', ch_acc[tt], acc[:, tt, :], t["x"][r0 + 512:r0 + 640, :], wait=[out_toks[tt]])
                ring32.release(wi, [tp])
            S6["actT_ready"] = tp

        for ti in range(4):
            wslots = [w_load32(t["w_out"][:, blk * 1024:(blk + 1) * 1024]) for blk in range(2)]
            x1_toks = [None] * 4
            for tt in range(4):
                for blk in range(2):
                    wi, slot, tw = wslots[blk]
                    for half in range(2):
                        b, bf = pb2.acquire()
                        tp = mm_group(bank(b), [(mtile[:, kc, tt * 128:(tt + 1) * 128], slot[:, kc, half * 512:(half + 1) * 512]) for kc in range(NKC)],
                                      waits=[tw, t_mt] + bf)
                        c0 = blk * 1024 + half * 512
                        i1, tmp, f1 = r_tmp.acquire()
                        t_e1 = op('dve', lambda e, tmp=tmp, b=b, c0=c0: e.tensor_tensor(out=tmp, in0=bank(b), in1=g1bc[:, c0:c0 + 512], op=ALU.mult),
                                  wait=[tp, t_gbc] + f1)
                        bank_free[b] = [t_e1]
                        t_e2 = op('dve', lambda e, tmp=tmp, tt=tt, c0=c0: e.tensor_tensor(out=acc[:, tt, c0:c0 + 512], in0=acc[:, tt, c0:c0 + 512], in1=tmp, op=ALU.add),
                                  wait=[t_e1, t_ax[tt]])
                        r_tmp.release(i1, [t_e2])
                x1_toks[tt] = t_e2
                if tt >= 1:
                    _, S6["t_h2"] = norm_tile(acc[:, tt - 1, :], [x1_toks[tt - 1]], A2col, B2col, [t_A2, t_B2], h2T[:, :, (tt - 1) * 128:tt * 128], xn2, pair2)
            S6["last_mt_read"] = tp
            for blk in range(2):
                ring32.release(wslots[blk][0], [tp])
            _, S6["t_h2"] = norm_tile(acc[:, 3, :], [x1_toks[3]], A2col, B2col, [t_A2, t_B2], h2T[:, :, 384:512], xn2, pair2)
            do_val(0, 0); do_gate(0, 0); do_val(0, 1); do_gate(0, 1)
            do_val(1, 0); do_wdown(0, ti); do_gate(1, 0); do_val(1, 1); do_gate(1, 1)
            do_val(2, 0); do_wdown(1, ti); do_gate(2, 0); do_val(2, 1); do_gate(2, 1)
            if ti < 3:
                nr = slice((ti + 1) * 512, (ti + 2) * 512)
                t_mt = P.dma('sp', ch_mt, mtile, mT[:, :, nr].rearrange("k p n -> p k n"), wait=[S6["t_act"], S6["last_mt_read"]])
            do_wdown(2, ti)
        P.wait_only('sp', out_toks)
        return finish()
    return nc


def _prep_inputs(inp, b):
    f = np.float32

    def colform(v, n):
        return np.ascontiguousarray(v.reshape(n, 128).T).astype(f)
    m = {}
    m["x"] = np.ascontiguousarray(inp["x"][b])
    m["cT"] = colform(inp["c"][b], 16)
    m["w_ada"] = inp["w_ada"][0]
    m["b_ada"] = inp["b_ada"][0].reshape(1, -1)
    m["g1col"] = colform(inp["g_norm1"][0], 16)
    m["g2col"] = colform(inp["g_norm2"][0], 16)
    m["w_in"] = inp["w_in"][0]
    rc = np.zeros((128, 8, 8), f)
    cw = inp["conv_rnn_w"][0]
    for j in range(4):
        rc[:, :, j] = colform(cw[j], 8)
    rc[:, :, 4] = colform(inp["conv_rnn_b"][0], 8)
    rc[:, :, 5] = colform(inp["b_rg_a"][0], 8)
    rc[:, :, 6] = colform(inp["b_rg_x"][0], 8)
    rc[:, :, 7] = colform(inp["lru_lambda"][0], 8)
    m["rnn_cols"] = rc.reshape(128, 64)
    for name, key in (("wa_bd", "w_rg_a"), ("wx_bd", "w_rg_x")):
        w = inp[key][0]
        bd = np.zeros((128, 8, 128), f)
        for c in range(8):
            for hh in range(2):
                bd[hh * 64:(hh + 1) * 64, c, hh * 64:(hh + 1) * 64] = w[2 * c + hh]
        m[name] = bd
    m["gqk"] = np.stack([inp["g_q"][0], inp["g_k"][0]], axis=1).astype(f)
    m["w_proj_rnn"] = inp["w_proj_rnn"][0]
    m["w_proj_sb"] = inp["w_proj_sb"][0]
    m["w_out"] = inp["w_out"][0]
    m["w_up"] = inp["w_up"][0]
    m["w_down"] = inp["w_down"][0]
    fc = np.zeros((128, 48, 4), f)
    fw = inp["conv_ffn_w"][0]
    for j in range(3):
        fc[:, :, j] = colform(fw[j], 48)
    fc[:, :, 3] = colform(inp["conv_ffn_b"][0], 48)
    m["ffn_cols"] = fc.reshape(128, 192)
    cst = np.zeros((128, 4, 128), f)
    cst[:, 0, :] = np.eye(128)
    jj, ss = np.meshgrid(np.arange(128), np.arange(128), indexing="ij")
    cst[:, 1, :] = -1.0 * (jj >= ss)
    cst[:, 2, :] = 1.0
    cst[:, 3, :] = (ss > jj)
    m["consts"] = cst
    return m


def kernel(**inputs):
    inp = {k: np.asarray(v) for k, v in inputs.items()}
    nc = build_nc()
    in_maps = [_prep_inputs(inp, b) for b in range(8)]
    res = run_bass_kernel_spmd(nc, in_maps, core_ids=list(range(8)))
    return np.stack([r["out"] for r in res.results], axis=0).astype(np.float32)
```

```python
import numpy as np
from contextlib import ExitStack
import concourse.bass as bass
import concourse.mybir as mybir
from concourse.bass_utils import run_bass_kernel_spmd

F32 = mybir.dt.float32
BF16 = mybir.dt.bfloat16
U8 = mybir.dt.uint8
AF = mybir.ActivationFunctionType
ALU = mybir.AluOpType

S = 2048
D = 2048
NKC = 16
DFF = 6144
EPS = 1e-6
ARENA_BYTES = 212000


class Chan:
    def __init__(self, sem, name):
        self.sem = sem
        self.n = 0
        self.name = name


class Prog:
    ENG = ('pe', 'act', 'dve', 'pool', 'sp')

    def __init__(self, nc, stack):
        self.nc = nc
        self.stack = stack
        self.q = {e: [] for e in self.ENG}
        self.done = {e: Chan(stack.enter_context(nc.semaphore("done_" + e)), "done_" + e) for e in self.ENG}
        self.waited = {e: {} for e in self.ENG}
        self.nchan = 0

    def dma_chan(self, name):
        self.nchan += 1
        return Chan(self.stack.enter_context(self.nc.semaphore(name)), name)

    def _filter(self, eng, wait):
        ws = []
        for tok in wait:
            if tok is None:
                continue
            ch, v = tok
            if self.waited[eng].get(ch.name, 0) >= v:
                continue
            self.waited[eng][ch.name] = v
            ws.append((ch, v))
        return ws

    def op(self, eng, fn, wait=(), sig=True):
        ws = self._filter(eng, wait)
        tok = None
        if sig:
            ch = self.done[eng]
            ch.n += 1
            tok = (ch, ch.n)
        dch = self.done[eng]

        def thunk(e, ws=ws, fn=fn, sig=sig, dch=dch):
            for ch, v in ws:
                e.wait_ge(ch.sem, v)
            ins = fn(e)
            if sig:
                ins.then_inc(dch.sem, 1)
        self.q[eng].append(thunk)
        return tok

    def dma(self, eng, chan, out, in_, wait=(), slow=False):
        ws = self._filter(eng, wait)
        chan.n += 16
        tok = (chan, chan.n)

        def thunk(e, ws=ws, out=out, in_=in_, chan=chan, slow=slow):
            for ch, v in ws:
                e.wait_ge(ch.sem, v)
            if slow:
                e.dma_start(out=out, in_=in_, allow_slow_non_contiguous=True).then_inc(chan.sem, 16)
            else:
                e.dma_start(out=out, in_=in_).then_inc(chan.sem, 16)
        self.q[eng].append(thunk)
        return tok

    def barrier(self, engs, chans=()):
        toks = [(self.done[e], self.done[e].n) for e in ('pe', 'act', 'dve') if self.done[e].n > 0]
        toks += [(ch, ch.n) for ch in chans if ch.n > 0]
        for e in engs:
            self.wait_only(e, toks)

    def wait_only(self, eng, wait):
        for ch, v in self._filter(eng, wait):
            self.q[eng].append(lambda e, ch=ch, v=v: e.wait_ge(ch.sem, v))

    def run(self, block):
        q = self.q

        @block.tensor
        def _(e):
            for t in q['pe']:
                t(e)

        @block.scalar
        def _(e):
            for t in q['act']:
                t(e)

        @block.vector
        def _(e):
            for t in q['dve']:
                t(e)

        @block.gpsimd
        def _(e):
            for t in q['pool']:
                t(e)

        @block.sync
        def _(e):
            for t in q['sp']:
                t(e)


class Arena:
    def __init__(self, ap_u8, size):
        self.ap = ap_u8
        self.size = size
        self.off = 0

    def alloc(self, shape, dt, off=None):
        n = int(np.prod(shape[1:])) * mybir.dt.size(dt)
        if off is None:
            off = self.off
            self.off += (n + 63) // 64 * 64
            assert self.off <= self.size, f"arena overflow {self.off} > {self.size}"
        else:
            assert off + n <= self.size, f"arena overflow (explicit) {off + n} > {self.size}"
        ap = self.ap[:, off:off + n].bitcast(dt)
        if len(shape) == 3:
            ap = ap.rearrange("p (a b) -> p a b", a=shape[1])
        if shape[0] < 128:
            ap = ap[0:shape[0]]
        return ap


class Ring:
    def __init__(self, aps, chans=None):
        self.aps = aps
        self.n = len(aps)
        self.free = [[] for _ in aps]
        self.chans = chans
        self.i = 0

    def acquire(self):
        i = self.i % self.n
        self.i += 1
        return i, self.aps[i], list(self.free[i])

    def release(self, i, toks):
        self.free[i] = [t for t in toks if t is not None]


def build_nc(dbg=(), stop_after=None):
    nc = bass.Bass("TRN2", target_bir_lowering=False)
    t = {}

    def din(name, shape, dt=F32):
        t[name] = nc.dram_tensor(name, shape, dt, kind="ExternalInput").ap()

    din("x", [S, D]); din("cT", [128, 16]); din("w_ada", [D, 6 * D]); din("b_ada", [1, 6 * D])
    din("g1col", [128, 16]); din("g2col", [128, 16]); din("w_in", [D, 9216])
    din("rnn_cols", [128, 64]); din("wa_bd", [128, 8, 128]); din("wx_bd", [128, 8, 128])
    din("gqk", [128, 2]); din("w_proj_rnn", [1024, D]); din("w_proj_sb", [1024, D]); din("w_out", [D, D])
    din("w_up", [D, 2 * DFF]); din("w_down", [DFF, D]); din("ffn_cols", [128, 192]); din("consts", [128, 4, 128])
    out = nc.dram_tensor("out", [S, D], F32, kind="ExternalOutput").ap()
    sg = nc.dram_tensor("sg_scr", [32, 128, S], F32).ap()
    mT = nc.dram_tensor("mT_scr", [16, 128, S], BF16).ap()
    gate_scr = nc.dram_tensor("gate_scr", [2, 128, D], F32).ap()
    col_scr = nc.dram_tensor("col_scr", [4, D], F32).ap()
    dbg_out = {}
    dbg_shapes = {"A1col": [128, 16], "B1col": [128, 16], "A2col": [128, 16], "B2col": [128, 16],
                  "hT": [128, 16 * S], "gT": [128, 8 * S], "oT": [128, 8 * S], "g1bc": [128, D], "g2bc": [128, D],
                  "qT0": [128, S], "kT0": [128, S], "v0": [128, S]}
    for name in dbg:
        dt_ = F32 if name in ("A1col", "B1col", "A2col", "B2col", "g1bc", "g2bc") else BF16
        dbg_out[name] = nc.dram_tensor("dbg_" + name, dbg_shapes[name], dt_, kind="ExternalOutput").ap()

    with ExitStack() as st:
        arena_t = st.enter_context(nc.sbuf_tensor("arena", [128, ARENA_BYTES], U8))
        ps_t = st.enter_context(nc.psum_tensor("ps", [128, 8, 512], F32))
        P = Prog(nc, st)
        A = Arena(arena_t, ARENA_BYTES)
        op = P.op

        def bank(b):
            return ps_t[:, b, :]
        bank_free = [[] for _ in range(8)]

        def mm_group(out_ap, pairs, waits=(), sig=True):
            n = len(pairs)
            tok = None
            for i, (l, r) in enumerate(pairs):
                tok = op('pe', lambda e, l=l, r=r, i=i: e.matmul(out_ap, lhsT=l, rhs=r, start=(i == 0), stop=(i == n - 1)),
                         wait=(waits if i == 0 else ()), sig=(sig and i == n - 1))
            return tok

        class BankRing:
            def __init__(self, ids):
                self.ids = ids
                self.i = 0

            def acquire(self):
                b = self.ids[self.i % len(self.ids)]
                self.i += 1
                return b, list(bank_free[b])

        cst_f = A.alloc([128, 4, 128], F32)
        identf = cst_f[:, 0, :]
        masku_f = cst_f[:, 3, :]
        identb = A.alloc([128, 128], BF16)
        tgeb = A.alloc([128, 128], BF16)
        onesb = A.alloc([128, 128], BF16)
        maskub = A.alloc([128, 128], BF16)
        onesnb = A.alloc([128, 128], BF16)
        SC = A.alloc([128, 16, 128], BF16)
        cTt = A.alloc([128, 16], F32); scol = A.alloc([128, 16], F32)
        g1c = A.alloc([128, 16], F32); g2c = A.alloc([128, 16], F32)
        modcols = [A.alloc([128, 16], F32) for _ in range(4)]
        A1col = A.alloc([128, 16], F32); A2col = A.alloc([128, 16], F32)
        B1col = modcols[0]; B2col = modcols[2]
        rnnc = A.alloc([128, 64], F32)
        rnnd = A.alloc([128, 32], F32)
        rtmp = A.alloc([128, 16], F32)
        gqk = A.alloc([128, 2], F32); gqs = A.alloc([128, 1], F32)
        ffnc = A.alloc([128, 192], F32)
        halo = A.alloc([128, 96], F32)
        small = [A.alloc([128, 4], F32) for _ in range(4)]
        RING_OFF = A.off
        A.off += 65536
        BIG_OFF = A.off
        BIG_SIZE = ARENA_BYTES - BIG_OFF
        ring32 = Ring([A.alloc([128, 16, 1024], BF16, off=RING_OFF + i * 32768) for i in range(2)])
        ring16 = Ring([A.alloc([128, 16, 512], BF16, off=RING_OFF + i * 16384) for i in range(4)])
        wchan = [P.dma_chan(f"w{i}") for i in range(4)]
        wchan32 = [P.dma_chan(f"wb{i}") for i in range(2)]

        def ring_mode16():
            for i in range(4):
                ring16.free[i] = list(ring32.free[i // 2])
            ring16.i = 0

        def ring_mode32():
            for i in range(2):
                ring32.free[i] = list(ring16.free[2 * i]) + list(ring16.free[2 * i + 1])
            ring32.i = 0

        ch_small = P.dma_chan("small")
        ch_dbg = P.dma_chan("dbg")
        dbg_toks = []

        def dump(name, ap, tok):
            if name in dbg_out:
                dbg_toks.append(P.dma('sp', ch_dbg, dbg_out[name], ap, wait=[tok]))

        for dst, src in [(cst_f, t["consts"]), (cTt, t["cT"]), (g1c, t["g1col"]), (g2c, t["g2col"]), (rnnc, t["rnn_cols"]),
                         (gqk, t["gqk"]), (ffnc, t["ffn_cols"])]:
            t_small = P.dma('sp', ch_small, dst, src)
        t0 = op('dve', lambda e: e.tensor_copy(out=identb, in_=cst_f[:, 0, :]), wait=[t_small])
        t0 = op('dve', lambda e: e.tensor_copy(out=tgeb, in_=cst_f[:, 1, :]))
        t0 = op('dve', lambda e: e.tensor_copy(out=onesb, in_=cst_f[:, 2, :]))
        t0 = op('dve', lambda e: e.tensor_scalar(out=onesnb, in0=cst_f[:, 2, :], scalar1=-1.0, scalar2=None, op0=ALU.mult))
        t_cst = op('dve', lambda e: e.tensor_copy(out=maskub, in_=cst_f[:, 3, :]))
        t_halo = op('dve', lambda e: e.memset(halo, 0.0))
        t_sc0 = op('act', lambda e: e.activation(out=scol, in_=cTt, func=AF.Silu), wait=[t_small])
        t_SC = op('dve', lambda e: e.tensor_copy(out=SC, in_=scol.unsqueeze(2).to_broadcast([128, 16, 128])), wait=[t_sc0])
        rc = rnnc.rearrange("p (c i) -> p c i", i=8)
        rd = rnnd.rearrange("p (c i) -> p c i", i=4)
        t1 = op('act', lambda e: e.activation(out=rtmp[:, 0:8], in_=rc[:, :, 7], func=AF.Exp, scale=-1.0), wait=[t_small])
        t1 = op('act', lambda e: e.activation(out=rtmp[:, 8:16], in_=rtmp[:, 0:8], func=AF.Ln, bias=1.0, scale=1.0), wait=[t1])
        t2 = op('dve', lambda e: e.tensor_scalar(out=rd[:, :, 0], in0=rtmp[:, 8:16], scalar1=-8.0, scalar2=None, op0=ALU.mult), wait=[t1])
        t2 = op('dve', lambda e: e.tensor_scalar(out=rd[:, :, 1], in0=rtmp[:, 8:16], scalar1=-4.0, scalar2=None, op0=ALU.mult))
        t2 = op('dve', lambda e: e.tensor_scalar(out=rd[:, :, 2], in0=rc[:, :, 5], scalar1=0.5, scalar2=None, op0=ALU.mult))
        t2 = op('dve', lambda e: e.tensor_scalar(out=rd[:, :, 3], in0=rc[:, :, 6], scalar1=0.5, scalar2=None, op0=ALU.mult))
        t_cols = op('dve', lambda e: e.tensor_scalar(out=gqs, in0=gqk[:, 0:1], scalar1=float(128 ** -0.5), scalar2=None, op0=ALU.mult))

        A.off = BIG_OFF
        segbufs = Ring([A.alloc([128, D], F32) for _ in range(2)])
        bsegs = Ring([A.alloc([128, D], F32) for _ in range(2)])
        ch_bseg = [P.dma_chan(f"bseg{i}") for i in range(2)]
        ch_gscr = [P.dma_chan(f"gscr{i}") for i in range(2)]
        ch_colw = [P.dma_chan(f"colw{i}") for i in range(4)]; ch_colr = [P.dma_chan(f"colr{i}") for i in range(4)]
        ada_banks = BankRing([0, 1, 2, 3, 4, 5])
        colbank = 6
        gscr_tok = [None, None]
        colseg = {0: 0, 1: 1, 3: 2, 4: 3}
        t_colcopy = {}
        def make_ada_items(segs, segbufs_, bsegs_, slot_fn, cols=1024, depth=1):
            nblk = D // cols
            nhalf = cols // 512
            blocks = [(seg, blk) for seg in segs for blk in range(nblk)]
            stt = {}
            wst = {}

            def loadw(k):
                seg, blk = blocks[k]
                c_ = seg * D + blk * cols
                wst[k] = slot_fn(t["w_ada"][:, c_:c_ + cols].rearrange("(kc p) n -> p kc n", p=128))
            items = []
            for k, (seg, blk) in enumerate(blocks):
                for half in range(nhalf):
                    def unit(k=k, seg=seg, blk=blk, half=half):
                        if blk == 0 and half == 0:
                            si, segbuf, seg_free = segbufs_.acquire()
                            bi, bseg, bseg_free = bsegs_.acquire()
                            t_b = P.dma('sp', ch_bseg[bi], bseg, t["b_ada"][0:1, seg * D:(seg + 1) * D].partition_broadcast(128), wait=bseg_free)
                            stt[seg] = dict(si=si, segbuf=segbuf, seg_free=seg_free, bi=bi, bseg=bseg, t_b=t_b, evs=[])
                        if k == 0 and half == 0:
                            for k0 in range(min(depth, len(blocks))):
                                loadw(k0)
                        s_ = stt[seg]
                        segbuf, bseg = s_["segbuf"], s_["bseg"]
                        rel, slot, t_w = wst[k]
                        b, bfree = ada_banks.acquire()
                        tok = mm_group(bank(b), [(SC[:, kc, :], slot[:, kc, half * 512:(half + 1) * 512]) for kc in range(NKC)],
                                       waits=[t_w, t_SC] + bfree)
                        c0 = blk * cols + half * 512
                        t_e = op('dve', lambda e, b=b, c0=c0, segbuf=segbuf, bseg=bseg: e.tensor_tensor(
                            out=segbuf[:, c0:c0 + 512], in0=bank(b), in1=bseg[:, c0:c0 + 512], op=ALU.add), wait=[tok, s_["t_b"]] + s_["seg_free"])
                        bank_free[b] = [t_e]
                        s_["evs"].append(t_e)
                        if half == nhalf - 1:
                            rel([tok])
                            if k + depth < len(blocks):
                                loadw(k + depth)
                        if blk == nblk - 1 and half == nhalf - 1:
                            evs = s_["evs"]
                            bsegs_.release(s_["bi"], [evs[-1]])
                            if seg in colseg:
                                ci = colseg[seg]
                                t_cw = P.dma('sp', ch_colw[ci], col_scr[ci:ci + 1, :], segbuf[0:1, :], wait=evs)
                                mc = modcols[ci]
                                t_cc = P.dma('sp', ch_colr[ci], mc, col_scr[ci:ci + 1, :].rearrange("o (kc p) -> p (o kc)", p=128), wait=[t_cw], slow=True)
                                t_colcopy[seg] = t_cc
                                segbufs_.release(s_["si"], [t_cw])
                            else:
                                gi = 0 if seg == 2 else 1
                                gscr_tok[gi] = P.dma('sp', ch_gscr[gi], gate_scr[gi], segbuf, wait=evs)
                                segbufs_.release(s_["si"], [gscr_tok[gi]])
                    items.append(unit)
            return items

        def slot32(src):
            wi, slot, wfree = ring32.acquire()
            t_w = P.dma('pool', wchan32[wi], slot, src, wait=wfree)
            return (lambda toks, wi=wi: ring32.release(wi, toks)), slot, t_w
        for it in make_ada_items([0, 1], segbufs, bsegs, slot32):
            it()
        t_A1 = op('dve', lambda e: e.scalar_tensor_tensor(out=A1col, in0=modcols[1], scalar=1.0, in1=g1c, op0=ALU.add, op1=ALU.mult),
                  wait=[t_colcopy[1], t_small])
        t_B1 = t_colcopy[0]
        dump("A1col", A1col, t_A1); dump("B1col", B1col, t_B1)

        def finish():
            P.wait_only('sp', dbg_toks[-1:])
            with nc.Block() as block:
                P.run(block)
            return nc

        if stop_after == "adaln":
            return finish()

        small_i = [0]
        xn_state = {}

        def norm_front(src, src_toks, xnring):
            k = small_i[0] % 4
            small_i[0] += 1
            ss = small[0][:, k:k + 1]; lnv = small[1][:, k:k + 1]; rstd = small[2][:, k:k + 1]
            xi, xn, xfree = xnring.acquire()
            t_sq = op('act', lambda e: e.activation(out=xn, in_=src, func=AF.Square, accum_out=ss), wait=list(src_toks) + xfree)
            t_ln = op('act', lambda e: e.activation(out=lnv, in_=ss, func=AF.Ln, scale=1.0 / D, bias=EPS), wait=[t_sq])
            t_rs = op('act', lambda e: e.activation(out=rstd, in_=lnv, func=AF.Exp, scale=-0.5), wait=[t_ln])
            t_xn = op('dve', lambda e: e.tensor_scalar(out=xn, in0=src, scalar1=rstd, scalar2=None, op0=ALU.mult), wait=[t_rs] + list(src_toks))
            return dict(xi=xi, xn=xn, t_xn=t_xn, rd_toks=[t_sq, t_xn], xnring=xnring)

        def norm_back(f, Acol, Bcol, AB_toks, dst, pairring):
            xn, t_xn, xnring = f["xn"], f["t_xn"], f["xnring"]
            pi, pb, pfree = pairring.acquire()
            for kc in range(NKC):
                t_tr = op('pe', lambda e, kc=kc: e.transpose(out=pb[:, kc, :], in_=xn[:, kc * 128:(kc + 1) * 128], identity=identb),
                          wait=([t_xn, t_cst] + pfree if kc == 0 else ()), sig=(kc == NKC - 1))
            t_ev = None; t_ev2 = None
            for kc in range(NKC):
                t_ev = op('act', lambda e, kc=kc: e.activation(out=dst[:, kc, :], in_=pb[:, kc, :], func=AF.Identity,
                                                              scale=Acol[:, kc:kc + 1], bias=Bcol[:, kc:kc + 1]),
                          wait=([t_tr] + list(AB_toks) if kc == 0 else ()), sig=(kc == NKC - 1))
            t_ev2 = t_ev
            pairring.release(pi, [t_ev, t_ev2])
            xnring.release(f["xi"], [t_tr])
            return [t_ev, t_ev2]

        def make_pairring(pairs):
            aps = []
            for (b0, b1) in pairs:
                aps.append(ps_t[:, b0:b1 + 1, :].bitcast(BF16).rearrange("p a (k n) -> p (a k) n", n=128))
            r = Ring(aps)
            r.pairs = pairs
            return r

        P.barrier(('act', 'dve', 'sp'), ch_gscr + ch_colw + ch_bseg)
        A.off = BIG_OFF
        hT = A.alloc([128, 16, S], BF16)
        P1_OFF = A.off
        xring = Ring([A.alloc([128, D], F32) for _ in range(2)])
        ch_x = [P.dma_chan(f"x{i}") for i in range(2)]
        xnring = Ring([A.alloc([128, D], BF16) for _ in range(2)])
        pairring = make_pairring([(0, 1), (2, 3)])
        for i, (b0, b1) in enumerate(pairring.pairs):
            pairring.free[i] = list(bank_free[b0]) + list(bank_free[b1])
        t_hT = []
        prev_n = None
        for tt in range(16):
            xi, xt, xfree = xring.acquire()
            t_x = P.dma('sp', ch_x[xi], xt, t["x"][tt * 128:(tt + 1) * 128, :], wait=xfree)
            f_ = norm_front(xt, [t_x], xnring)
            xring.release(xi, f_["rd_toks"])
            if prev_n is not None:
                t_hT = norm_back(prev_n[0], A1col, B1col, [t_A1, t_B1], prev_n[1], pairring)
            prev_n = (f_, hT[:, :, tt * 128:(tt + 1) * 128])
        t_hT = norm_back(prev_n[0], A1col, B1col, [t_A1, t_B1], prev_n[1], pairring)
        for i, (b0, b1) in enumerate(pairring.pairs):
            bank_free[b0] = list(pairring.free[i]); bank_free[b1] = list(pairring.free[i])
        dump("hT", hT.rearrange("p a b -> p (a b)"), t_hT[0])
        if stop_after == "norm1":
            return finish()

        P.barrier(('act', 'dve', 'sp'), ch_x)
        A.off = P1_OFF
        gT = A.alloc([128, 8, S], BF16)
        P2_OFF = A.off
        wab = A.alloc([128, 8, 128], BF16)
        wxb = A.alloc([128, 8, 128], BF16)
        ch_wbd = P.dma_chan("wbd")
        P.dma('pool', ch_wbd, wab, t["wa_bd"])
        t_wab = P.dma('pool', ch_wbd, wxb, t["wx_bd"])
        t_wxb = t_wab

        def ring_of(n, shape, dt):
            return Ring([A.alloc(shape, dt) for _ in range(n)])
        XB = ring_of(2, [128, 515], F32)
        r_xr = ring_of(2, [128, 512], F32); r_xrb = ring_of(2, [128, 512], BF16)
        r_thr = ring_of(2, [128, 512], F32); r_a = ring_of(2, [128, 512], F32); r_a2 = ring_of(2, [128, 512], F32)
        r_thi = ring_of(2, [128, 512], F32); r_hr = ring_of(2, [128, 512], F32); r_ge = ring_of(2, [128, 512], BF16)
        ring_mode16()
        rnn_w = []
        for (c0_, si_) in ((0, 0), (1024, 1), (512, 2), (1536, 3)):
            wi, slot, wfree = ring16.acquire()
            tw = P.dma('pool', wchan[wi], slot, t["w_in"][:, c0_:c0_ + 512].rearrange("(kc p) n -> p kc n", p=128), wait=wfree)
            rnn_w.append((wi, slot, tw))
        rb = BankRing([0, 1, 2, 3, 4, 5, 6, 7])
        last_pe_w = [None] * 4
        prev_xb = None
        prev_hr = None
        t_gT = None
        for c in range(8):
            wiX, Xs, tXw = rnn_w[0 if c < 4 else 2]
            wiG, Gs, tGw = rnn_w[1 if c < 4 else 3]
            cc = c % 4
            colw = lambda j, c=c: rc[:, c, j:j + 1]
            for tb in range(4):
                tsl = slice(tb * 512, (tb + 1) * 512)
                bX, fX = rb.acquire()
                tX = mm_group(bank(bX), [(Xs[:, kc, cc * 128:(cc + 1) * 128], hT[:, kc, tsl]) for kc in range(NKC)], waits=[tXw] + t_hT + fX)
                bG, fG = rb.acquire()
                tG = mm_group(bank(bG), [(Gs[:, kc, cc * 128:(cc + 1) * 128], hT[:, kc, tsl]) for kc in range(NKC)], waits=[tGw] + fG)
                last_pe_w[0 if c < 4 else 2] = tX; last_pe_w[1 if c < 4 else 3] = tG
                xbi, xb, xbfree = XB.acquire()
                if tb == 0:
                    t_h = op('dve', lambda e, xb=xb: e.memset(xb[:, 0:3], 0.0), wait=xbfree)
                else:
                    t_h = op('dve', lambda e, xb=xb, pxb=prev_xb[0]: e.tensor_copy(out=xb[:, 0:3], in_=pxb[:, 512:515]), wait=xbfree + [prev_xb[1]])
                t_xc = op('act', lambda e, xb=xb, bX=bX: e.activation(out=xb[:, 3:515], in_=bank(bX), func=AF.Copy), wait=[tX] + xbfree)
                bank_free[bX] = [t_xc]
                prev_xb = (xb, t_xc)
                i1, xr, f1 = r_xr.acquire()
                t_c = op('dve', lambda e, xr=xr, xb=xb, c=c: e.tensor_scalar(out=xr, in0=xb[:, 3:515], scalar1=rc[:, c, 3:4], scalar2=rc[:, c, 4:5],
                                                                         op0=ALU.mult, op1=ALU.add), wait=[t_xc, t_h, t_small] + f1)
                for j in (2, 1, 0):
                    t_c = op('dve', lambda e, xr=xr, xb=xb, c=c, j=j: e.scalar_tensor_tensor(out=xr, in0=xb[:, j:j + 512], scalar=rc[:, c, j:j + 1], in1=xr,
                                                                                          op0=ALU.mult, op1=ALU.add), wait=[t_c])
                XB.release(xbi, [t_c])
                i2, xrb, f2 = r_xrb.acquire()
                t_xrb = op('act', lambda e, xrb=xrb, xr=xr: e.activation(out=xrb, in_=xr, func=AF.Copy), wait=[t_c] + f2)
                bR, fR = rb.acquire()
                tR = op('pe', lambda e, bR=bR, c=c, xrb=xrb: e.matmul(bank(bR), lhsT=wab[:, c, :], rhs=xrb, start=True, stop=True), wait=[t_xrb, t_wab] + fR)
                bI, fI = rb.acquire()
                tI = op('pe', lambda e, bI=bI, c=c, xrb=xrb: e.matmul(bank(bI), lhsT=wxb[:, c, :], rhs=xrb, start=True, stop=True), wait=[t_wxb] + fI)
                r_xrb.release(i2, [tI])
                i3, thr, f3 = r_thr.acquire()
                t_thr = op('act', lambda e, thr=thr, bR=bR, c=c: e.activation(out=thr, in_=bank(bR), func=AF.Tanh, scale=0.5, bias=rd[:, c, 2:3]), wait=[tR, t_cols] + f3)
                bank_free[bR] = [t_thr]
                i4, av, f4 = r_a.acquire()
                t_a = op('act', lambda e, av=av, thr=thr, c=c: e.activation(out=av, in_=thr, func=AF.Exp, scale=rd[:, c, 1:2], bias=rd[:, c, 1:2]), wait=[t_thr] + f4)
                i5, a2, f5 = r_a2.acquire()
                t_a2 = op('act', lambda e, a2=a2, thr=thr, c=c: e.activation(out=a2, in_=thr, func=AF.Exp, scale=rd[:, c, 0:1], bias=rd[:, c, 0:1]), wait=[t_thr] + f5)
                r_thr.release(i3, [t_a2])
                i6, thi, f6 = r_thi.acquire()
                t_thi = op('act', lambda e, thi=thi, bI=bI, c=c: e.activation(out=thi, in_=bank(bI), func=AF.Tanh, scale=0.5, bias=rd[:, c, 3:4]), wait=[tI] + f6)
                bank_free[bI] = [t_thi]
                t_u = op('dve', lambda e, a2=a2: e.tensor_scalar(out=a2, in0=a2, scalar1=1.0, scalar2=-1.0, op0=ALU.min, op1=ALU.mult), wait=[t_a2])
                t_s = op('act', lambda e, a2=a2: e.activation(out=a2, in_=a2, func=AF.Sqrt, bias=1.0, scale=1.0), wait=[t_u])
                t_b = op('dve', lambda e, thi=thi, xr=xr: e.scalar_tensor_tensor(out=thi, in0=thi, scalar=1.0, in1=xr, op0=ALU.add, op1=ALU.mult), wait=[t_thi, t_c])
                r_xr.release(i1, [t_b, t_xrb])
                t_b = op('dve', lambda e, thi=thi, a2=a2: e.scalar_tensor_tensor(out=thi, in0=thi, scalar=0.5, in1=a2, op0=ALU.mult, op1=ALU.mult), wait=[t_b, t_s])
                r_a2.release(i5, [t_b])
                i7, hr, f7 = r_hr.acquire()
                if tb == 0:
                    t_hr = op('dve', lambda e, hr=hr, av=av, thi=thi: e.tensor_tensor_scan(out=hr, data0=av, data1=thi, initial=0.0, op0=ALU.mult, op1=ALU.add),
                              wait=[t_a, t_b] + f7)
                else:
                    t_hr = op('dve', lambda e, hr=hr, av=av, thi=thi, ph=prev_hr[0]: e.tensor_tensor_scan(out=hr, data0=av, data1=thi, initial=ph[:, 511:512],
                                                                                                         op0=ALU.mult, op1=ALU.add),
                              wait=[t_a, t_b, prev_hr[1]] + f7)
                r_a.release(i4, [t_hr]); r_thi.release(i6, [t_hr])
                i8, ge, f8 = r_ge.acquire()
                t_ge = op('act', lambda e, ge=ge, bG=bG: e.activation(out=ge, in_=bank(bG), func=AF.Gelu_apprx_tanh), wait=[tG] + f8)
                bank_free[bG] = [t_ge]
                t_gT = op('dve', lambda e, ge=ge, hr=hr, c=c, tsl=tsl: e.tensor_tensor(out=gT[:, c, tsl], in0=ge, in1=hr, op=ALU.mult), wait=[t_ge, t_hr])
                r_ge.release(i8, [t_gT])
                if prev_hr is not None:
                    r_hr.release(prev_hr[2], [t_hr, prev_hr[3]])
                prev_hr = (hr, t_hr, i7, t_gT)
            r_hr.release(prev_hr[2], [prev_hr[3]])
            prev_hr = None
            prev_xb = None
        for k_, (wi, slot, tw) in enumerate(rnn_w):
            ring16.release(wi, [last_pe_w[k_]])
        dump("gT", gT.rearrange("p a b -> p (a b)"), t_gT)
        if stop_after == "rnn":
            return finish()

        P.barrier(('act', 'dve', 'sp', 'pool'), wchan + wchan32 + [ch_wbd])
        A.off = P2_OFF
        oT = A.alloc([128, 8, S], BF16)
        P3_OFF = A.off
        stg = Ring([A.alloc([128, 512], F32) for _ in range(2)])
        ch_stg = [P.dma_chan(f"stg{i}") for i in range(2)]
        AR = Arena(arena_t, ARENA_BYTES)
        AR.off = RING_OFF

        def ring_r(n, shape, dt):
            r = Ring([AR.alloc(shape, dt) for _ in range(n)])
            assert AR.off <= RING_OFF + 49152, "ring region overflow"
            return r
        Wq = ring_r(1, [128, 16, 128], BF16); Wk = ring_r(1, [128, 16, 128], BF16); Wv = ring_r(1, [128, 16, 128], BF16)
        ch_hw = [[P.dma_chan(f"hw{j}{i}") for i in range(1)] for j in range(3)]
        qT = AR.alloc([128, S], BF16); kT = AR.alloc([128, S], BF16); vh = AR.alloc([128, 16, 128], BF16)
        r_sq = ring_r(2, [128, 512], BF16); r_ln = ring_r(2, [128, 512], F32)
        r_E = ring_r(2, [128, 512], F32); r_SP = ring_r(2, [128, 512], BF16); r_X = ring_r(2, [128, 512], F32)
        r_W = ring_r(2, [128, 512], BF16); r_sum = ring_r(2, [128, 512], BF16)
        r_g1 = ring_r(1, [128, 512], F32)
        gslot = ring16.aps[3]
        pj = BankRing([0, 1]); ssb = BankRing([2, 3])
        zbr = BankRing([0, 1]); bbr = BankRing([2, 3]); otr = BankRing([4, 5]); gbr = BankRing([6, 7])
        qkv_free = {"q": [], "k": [], "v": []}
        t_oT = None
        sg_tok = {}
        import collections
        gate_items = collections.deque()

        def make_gate_block(blk):
            state = {}

            def load():
                state["tw"] = P.dma('pool', wchan[3], gslot, t["w_in"][:, 5120 + blk * 512: 5120 + (blk + 1) * 512].rearrange("(kc p) n -> p kc n", p=128),
                                    wait=list(state.get("free", [])))
            items = [load]
            for oci in range(4):
                for tb in range(4):
                    gst = {}

                    def half1(oci=oci, tb=tb, gst=gst):
                        tsl = slice(tb * 512, (tb + 1) * 512)
                        b, bf = gbr.acquire()
                        gst["b"] = b
                        for kc in range(8):
                            op('pe', lambda e, b=b, kc=kc, oci=oci, tsl=tsl: e.matmul(bank(b), lhsT=gslot[:, kc, oci * 128:(oci + 1) * 128], rhs=hT[:, kc, tsl],
                                                                                   start=(kc == 0), stop=False),
                               wait=([state["tw"]] + bf if kc == 0 else ()), sig=False)

                    def half2(oci=oci, tb=tb, gst=gst):
                        oc = blk * 4 + oci
                        tsl = slice(tb * 512, (tb + 1) * 512)
                        b = gst["b"]
                        for kc in range(8, 16):
                            tp = op('pe', lambda e, b=b, kc=kc, oci=oci, tsl=tsl: e.matmul(bank(b), lhsT=gslot[:, kc, oci * 128:(oci + 1) * 128], rhs=hT[:, kc, tsl],
                                                                                        start=False, stop=(kc == 15)), sig=(kc == 15))
                        i1, g1, f1 = r_g1.acquire()
                        t_e = op('act', lambda e, g1=g1, b=b: e.activation(out=g1, in_=bank(b), func=AF.Exp, scale=-1.0), wait=[tp] + f1)
                        bank_free[b] = [t_e]
                        t_p = op('act', lambda e, g1=g1: e.activation(out=g1, in_=g1, func=AF.Ln, bias=1.0, scale=1.0), wait=[t_e])
                        si, sb_, sfree = stg.acquire()
                        t_s = op('act', lambda e, sb_=sb_, g1=g1: e.activation(out=sb_, in_=g1, func=AF.Exp, scale=-1.0), wait=[t_p] + sfree)
                        r_g1.release(i1, [t_s])
                        t_st = P.dma('sp', ch_stg[si], sg[oc][:, tsl], sb_, wait=[t_s])
                        stg.release(si, [t_st])
                        sg_tok[(oc, tb)] = t_st
                        state["last"] = tp
                    items.append(half1)
                    items.append(half2)
            return items, state
        gate_states = []
        for blk in range(8):
            items, stt_ = make_gate_block(blk)
            gate_states.append(stt_)
            gate_items.append(items)

        def load_head_w(h):
            res = []
            for j, (Wr, base) in enumerate(((Wq, 2048), (Wk, 3072), (Wv, 4096))):
                wi, w, wfree = Wr.acquire()
                tw = P.dma('pool', ch_hw[j][wi], w, t["w_in"][:, base + h * 128: base + (h + 1) * 128].rearrange("(kc p) n -> p kc n", p=128), wait=wfree)
                res.append((wi, w, tw))
            return res
        hw_next = load_head_w(0)
        for h in range(8):
            hw = hw_next
            my_gate = list(gate_items.popleft())
            if h > 0:
                gate_states[h]["free"] = [gate_states[h - 1]["last"]]
            my_gate.pop(0)()
            for j, (dstT, gcol, key) in enumerate(((qT, gqs[:, 0:1], "q"), (kT, gqk[:, 1:2], "k"))):
                wi, w, tw = hw[j]
                for tb in range(4):
                    tsl = slice(tb * 512, (tb + 1) * 512)
                    b, bf = pj.acquire()
                    tp = mm_group(bank(b), [(w[:, kc, :], hT[:, kc, tsl]) for kc in range(NKC)], waits=[tw] + bf)
                    last_w = tp
                    i1, sq, f1 = r_sq.acquire()
                    t_sq = op('act', lambda e, sq=sq, b=b: e.activation(out=sq, in_=bank(b), func=AF.Square), wait=[tp] + f1)
                    b2, bf2 = ssb.acquire()
                    t_ss = op('pe', lambda e, b2=b2, sq=sq: e.matmul(bank(b2), lhsT=onesb, rhs=sq, start=True, stop=True), wait=[t_sq, t_cst] + bf2)
                    r_sq.release(i1, [t_ss])
                    i2, ln, f2 = r_ln.acquire()
                    t_ln = op('act', lambda e, ln=ln, b2=b2: e.activation(out=ln, in_=bank(b2), func=AF.Ln, scale=1.0 / 128, bias=EPS), wait=[t_ss] + f2)
                    bank_free[b2] = [t_ln]
                    t_R = op('act', lambda e, ln=ln: e.activation(out=ln, in_=ln, func=AF.Exp, scale=-0.5), wait=[t_ln])
                    t_q = op('dve', lambda e, dstT=dstT, b=b, gcol=gcol, ln=ln, tsl=tsl: e.scalar_tensor_tensor(out=dstT[:, tsl], in0=bank(b), scalar=gcol, in1=ln,
                                                                                                        op0=ALU.mult, op1=ALU.mult),
                              wait=[tp, t_R, t_cols] + qkv_free[key])
                    bank_free[b] = [t_q]
                    r_ln.release(i2, [t_q])
                (Wq if j == 0 else Wk).release(wi, [last_w])
                if j == 0:
                    t_qT = t_q
                else:
                    t_kT = t_q
            wi, w, tw = hw[2]
            for tg in range(4):
                b, bf = pj.acquire()
                for tti in range(4):
                    tt = tg * 4 + tti
                    tp = mm_group(bank(b)[:, tti * 128:(tti + 1) * 128], [(hT[:, kc, tt * 128:(tt + 1) * 128], w[:, kc, :]) for kc in range(NKC)],
                                  waits=([tw] + bf if tti == 0 else ()), sig=(tti == 3))
                t_v = op('dve', lambda e, b=b, tg=tg: e.tensor_copy(out=vh[:, tg * 4:(tg + 1) * 4, :].rearrange("p a b -> p (a b)"), in_=bank(b)),
                         wait=[tp] + qkv_free["v"])
                bank_free[b] = [t_v]
            Wv.release(wi, [tp])
            if h + 1 < 8:
                hw_next = load_head_w(h + 1)
            units = [(qc, sb) for qc in range(4) for sb in range(4 * qc + 3, -1, -1)]
            st = {}
            last_read = {"q": None, "k": None, "v": None}

            def unit_a1(qc, sb):
                c0 = max(0, 128 * (sb - 4 * qc)); N = 512 - c0; qa = 512 * qc + c0; diag = sb >= 4 * qc
                first = (sb == 4 * qc + 3)
                if first:
                    ci, ssum, cfree = r_sum.acquire()
                    t_cm = op('dve', lambda e, ssum=ssum: e.memset(ssum, 0.0), wait=cfree)
                    bo, bof = otr.acquire()
                    st[qc] = {"sum": ssum, "ci": ci, "t_sum": t_cm, "ot": bo, "otfree": bof}
                ksl = slice(sb * 128, (sb + 1) * 128); qsl = slice(qa, qa + N)
                bz, fz = zbr.acquire()
                t_z = op('pe', lambda e, bz=bz, ksl=ksl, qsl=qsl, N=N: e.matmul(bank(bz)[:, 0:N], lhsT=kT[:, ksl], rhs=qT[:, qsl], start=True, stop=True),
                         wait=[t_qT, t_kT] + fz)
                i1, E, f1 = r_E.acquire()
                t_E = op('act', lambda e, E=E, bz=bz, N=N: e.activation(out=E[:, 0:N], in_=bank(bz)[:, 0:N], func=AF.Exp), wait=[t_z] + f1)
                bank_free[bz] = [t_E]
                i2, SP, f2 = r_SP.acquire()
                t_SP = op('act', lambda e, SP=SP, E=E, N=N: e.activation(out=SP[:, 0:N], in_=E[:, 0:N], func=AF.Ln, bias=1.0, scale=1.0), wait=[t_E] + f2)
                if diag:
                    t_SP = op('dve', lambda e, SP=SP: e.tensor_tensor(out=SP[:, 0:128], in0=SP[:, 0:128], in1=maskub, op=ALU.mult), wait=[t_SP, t_cst])
                last_read["q"] = t_z; last_read["k"] = t_z
                return dict(qc=qc, sb=sb, c0=c0, N=N, diag=diag, first=first, E=E, iE=i1, SP=SP, iSP=i2, t_SP=t_SP, t_E=t_E)

            def unit_a2(u):
                qc, sb, c0, N, SP = u["qc"], u["sb"], u["c0"], u["N"], u["SP"]
                s_ = st[qc]; ssum = s_["sum"]
                bb_, fb = bbr.acquire()
                first = u["first"]
                t_B = op('pe', lambda e, bb_=bb_, SP=SP, N=N, first=first: e.matmul(bank(bb_)[:, 0:N], lhsT=tgeb, rhs=SP[:, 0:N], start=True, stop=first),
                         wait=[u["t_SP"]] + fb)
                if not first:
                    t_B = op('pe', lambda e, bb_=bb_, ssum=ssum, N=N, c0=c0: e.matmul(bank(bb_)[:, 0:N], lhsT=onesnb, rhs=ssum[:, c0:512], start=False, stop=True),
                             wait=[s_["t_sum"]])
                t_su = op('dve', lambda e, ssum=ssum, SP=SP, N=N, c0=c0: e.tensor_tensor(out=ssum[:, c0:512], in0=ssum[:, c0:512], in1=SP[:, 0:N], op=ALU.add),
                          wait=[t_B, u["t_SP"], s_["t_sum"]])
                s_["t_sum"] = t_su
                r_SP.release(u["iSP"], [t_su, t_B])
                u["bb"] = bb_; u["t_B"] = t_B

            def unit_b1(u):
                N, c0, E = u["N"], u["c0"], u["E"]
                bb_ = u["bb"]
                i0, X, f0 = r_X.acquire()
                t_X = op('act', lambda e, X=X, bb_=bb_, N=N: e.activation(out=X[:, 0:N], in_=bank(bb_)[:, 0:N], func=AF.Exp), wait=[u["t_B"]] + f0)
                bank_free[bb_] = [t_X]
                i1, W, f1 = r_W.acquire()
                first = u["first"]
                wo = c0 if first else 0
                t_z0 = None
                if first:
                    t_z0 = op('dve', lambda e, W=W, c0=c0: e.memset(W[:, 0:c0], 0.0), wait=f1)
                t_W = op('dve', lambda e, W=W, X=X, E=E, N=N, wo=wo: e.tensor_tensor(out=W[:, wo:wo + N], in0=E[:, 0:N], in1=X[:, 0:N], op=ALU.mult),
                         wait=[t_X, u["t_E"]] + f1)
                r_X.release(i0, [t_W]); r_E.release(u["iE"], [t_W])
                if u["diag"]:
                    t_W = op('dve', lambda e, W=W, wo=wo: e.tensor_tensor(out=W[:, wo:wo + 128], in0=W[:, wo:wo + 128], in1=maskub, op=ALU.mult), wait=[t_W])
                u["W"] = W; u["iW"] = i1; u["t_W"] = t_W; u["t_z0"] = t_z0

            def unit_b2(u):
                nonlocal t_oT
                qc, sb, c0, N, W = u["qc"], u["sb"], u["c0"], u["N"], u["W"]
                s_ = st[qc]
                bo = s_["ot"]
                if u["first"]:
                    t_O = op('pe', lambda e, bo=bo, W=W, sb=sb: e.matmul(bank(bo), lhsT=vh[:, sb, :], rhs=W, start=True, stop=(sb == 0)),
                             wait=[u["t_W"], u["t_z0"], t_v] + s_["otfree"])
                else:
                    t_O = op('pe', lambda e, bo=bo, W=W, sb=sb, N=N, c0=c0: e.matmul(bank(bo)[:, c0:512], lhsT=vh[:, sb, :], rhs=W[:, 0:N],
                                                                                    start=False, stop=(sb == 0)), wait=[u["t_W"], t_v])
                r_W.release(u["iW"], [t_O])
                last_read["v"] = t_O
                if sb == 0:
                    t_oT = op('act', lambda e, bo=bo, h=h, qc=qc: e.activation(out=oT[:, h, qc * 512:(qc + 1) * 512], in_=bank(bo), func=AF.Copy), wait=[t_O])
                    bank_free[bo] = [t_oT]
                    r_sum.release(s_["ci"], [s_["t_sum"]])

            prev = None
            for ui, (qc, sb) in enumerate(units):
                if prev is not None:
                    unit_b1(prev)
                u = unit_a1(qc, sb)
                if my_gate:
                    my_gate.pop(0)()
                if prev is not None:
                    unit_b2(prev)
                unit_a2(u)
                prev = u
            unit_b1(prev); unit_b2(prev)
            while my_gate:
                my_gate.pop(0)()
            qkv_free["q"] = [last_read["q"]]; qkv_free["k"] = [last_read["k"]]; qkv_free["v"] = [last_read["v"]]
        dump("oT", oT.rearrange("p a b -> p (a b)"), t_oT)
        if stop_after == "attn":
            return finish()
        gb = BankRing([0, 1, 2, 3, 4, 5, 6, 7])
        ring16.free = [[] for _ in range(4)]
        ring16.free[3] = [gate_states[7]["last"]]
        ring16.i = 0

        P.barrier(('act', 'dve', 'sp', 'pool'), ch_stg + [c_ for row in ch_hw for c_ in row] + wchan)
        A.off = BIG_OFF
        sgr_r = Ring([A.alloc([128, S], F32) for _ in range(2)]); sgs_r = Ring([A.alloc([128, S], F32) for _ in range(2)])
        ch_sgr = [P.dma_chan(f"sgr{i}") for i in range(2)]; ch_sgs = [P.dma_chan(f"sgs{i}") for i in range(2)]
        r_t1 = Ring([A.alloc([128, 512], F32) for _ in range(2)]); r_t2 = Ring([A.alloc([128, 512], F32) for _ in range(2)])
        mst = Ring([A.alloc([128, 512], BF16) for _ in range(4)])
        ch_mst = [P.dma_chan(f"mst{i}") for i in range(4)]
        assert A.off <= BIG_OFF + 65536
        mT_tok = {}
        segbufs2 = Ring([A.alloc([128, D], F32)]); bsegs2 = Ring([A.alloc([128, D], F32)])
        assert A.off <= BIG_OFF + 65536
        aslot = ring32.aps[1]
        aslot_state = {"free": list(ring16.free[2]) + list(ring16.free[3])}

        def slot_single(src):
            t_w = P.dma('pool', wchan32[1], aslot, src, wait=aslot_state["free"])

            def rel(toks):
                aslot_state["free"] = list(toks)
            return rel, aslot, t_w
        ada_banks = gb
        ada2 = make_ada_items([2, 3, 4, 5], segbufs2, bsegs2, slot_single)
        ring16m = Ring(ring16.aps[0:2])
        ring16m.free = [list(ring16.free[0]), list(ring16.free[1])]

        def load_pj(blk):
            wi, slot, wfree = ring16m.acquire()
            P.dma('pool', wchan[wi], slot[:, 0:8, :], t["w_proj_rnn"][:, blk * 512:(blk + 1) * 512].rearrange("(kc p) n -> p kc n", p=128), wait=wfree)
            tw = P.dma('pool', wchan[wi], slot[:, 8:16, :], t["w_proj_sb"][:, blk * 512:(blk + 1) * 512].rearrange("(kc p) n -> p kc n", p=128))
            return wi, slot, tw
        pj_loaded = {0: load_pj(0)}
        nunit = 0
        for blk in range(4):
            wi, slot, tw = pj_loaded[blk]
            if blk + 1 < 4 and blk >= 1:
                pass
            for oci in range(4):
                oc = blk * 4 + oci
                ri, sgr_t, rfree = sgr_r.acquire()
                t_lr = P.dma('sp', ch_sgr[ri], sgr_t, sg[oc], wait=rfree + [sg_tok[(oc, 3)]])
                si2, sgs_t, sfree2 = sgs_r.acquire()
                t_ls = P.dma('sp', ch_sgs[si2], sgs_t, sg[16 + oc], wait=sfree2 + [sg_tok[(16 + oc, 3)]])
                for tb in range(4):
                    tsl = slice(tb * 512, (tb + 1) * 512)
                    b1, bf1 = gb.acquire()
                    tp1 = mm_group(bank(b1), [(slot[:, kc, oci * 128:(oci + 1) * 128], gT[:, kc, tsl]) for kc in range(8)], waits=[tw, t_gT] + bf1)
                    b2, bf2 = gb.acquire()
                    tp2 = mm_group(bank(b2), [(slot[:, 8 + kc, oci * 128:(oci + 1) * 128], oT[:, kc, tsl]) for kc in range(8)], waits=[t_oT] + bf2)
                    i1, t1_, f1 = r_t1.acquire()
                    t_a1 = op('dve', lambda e, t1_=t1_, b1=b1, sgr_t=sgr_t, tsl=tsl: e.tensor_tensor(out=t1_, in0=bank(b1), in1=sgr_t[:, tsl], op=ALU.mult),
                              wait=[tp1, t_lr] + f1)
                    bank_free[b1] = [t_a1]
                    i2, t2_, f2 = r_t2.acquire()
                    t_a2 = op('dve', lambda e, t2_=t2_, b2=b2, sgs_t=sgs_t, tsl=tsl: e.tensor_tensor(out=t2_, in0=bank(b2), in1=sgs_t[:, tsl], op=ALU.mult),
                              wait=[tp2, t_ls] + f2)
                    bank_free[b2] = [t_a2]
                    mi, m_, mfree = mst.acquire()
                    t_m = op('dve', lambda e, m_=m_, t1_=t1_, t2_=t2_: e.tensor_tensor(out=m_, in0=t1_, in1=t2_, op=ALU.add), wait=[t_a1, t_a2] + mfree)
                    r_t1.release(i1, [t_m]); r_t2.release(i2, [t_m])
                    t_ms = P.dma('sp', ch_mst[mi], mT[oc][:, tsl], m_, wait=[t_m])
                    mst.release(mi, [t_ms])
                    mT_tok[(oc, tb)] = t_ms
                    nunit += 1
                    if nunit % 16 == 2 and blk + 1 < 4:
                        pj_loaded[blk + 1] = load_pj(blk + 1)
                    if nunit % 4 == 0 and ada2:
                        ada2.pop(0)()
                sgr_r.release(ri, [t_a1]); sgs_r.release(si2, [t_a2])
            ring16m.release(wi, [tp2])
            nunit = 0 if False else nunit
            if blk + 1 < 4 and (blk + 1) not in pj_loaded:
                pj_loaded[blk + 1] = load_pj(blk + 1)
        while ada2:
            ada2.pop(0)()
        t_A2 = op('dve', lambda e: e.scalar_tensor_tensor(out=A2col, in0=modcols[3], scalar=1.0, in1=g2c, op0=ALU.add, op1=ALU.mult),
                  wait=[t_colcopy[4], t_small])
        t_B2 = t_colcopy[3]
        ring16.free[0] = list(ring16m.free[0]); ring16.free[1] = list(ring16m.free[1])
        ring16.free[2] = list(aslot_state["free"]); ring16.free[3] = list(aslot_state["free"])
        if stop_after == "merge":
            P.wait_only('sp', [mT_tok[(15, tb)] for tb in range(4)])
            return finish()

        P.barrier(('act', 'dve', 'sp', 'pool'), ch_mst + ch_sgr + ch_sgs + wchan + wchan32 + ch_bseg + ch_gscr + ch_colw)
        ring_mode32()
        A.off = BIG_OFF
        acc = A.alloc([128, 4, D], F32)
        h2T = A.alloc([128, 16, 512], BF16)
        actT = A.alloc([128, 16, 512], BF16)
        mtile = A.alloc([128, 16, 512], BF16)
        valbuf = mtile.bitcast(F32) if False else None
        VAL_OFF = A.off - 16384
        valbuf = A.alloc([128, 8, 512], F32, off=VAL_OFF)
        g1bc = A.alloc([128, D], F32); g2bc = A.alloc([128, D], F32)
        xn2 = Ring([A.alloc([128, D], BF16) for _ in range(2)])
        r_tmp = Ring([A.alloc([128, 512], F32) for _ in range(2)])
        r_G = Ring([A.alloc([128, 514], F32) for _ in range(2)])
        r_ca = Ring([A.alloc([128, 512], F32) for _ in range(2)])
        r_ge2 = Ring([A.alloc([128, 512], F32) for _ in range(2)])
        ch_g = P.dma_chan("gbc"); ch_mt = P.dma_chan("mtile")
        ch_acc = [P.dma_chan(f"accx{i}") for i in range(4)]; ch_out = [P.dma_chan(f"outst{i}") for i in range(4)]
        P.dma('sp', ch_g, g1bc, gate_scr[0], wait=[gscr_tok[0]])
        t_gbc = P.dma('sp', ch_g, g2bc, gate_scr[1], wait=[gscr_tok[1]])
        fc = ffnc.rearrange("p (c i) -> p c i", i=4)
        hl = halo.rearrange("p (c i) -> p c i", i=2)
        pb2 = BankRing([0, 1, 2, 3, 4, 5])
        pair2 = make_pairring([(6, 7)])
        pair2.free[0] = list(bank_free[6]) + list(bank_free[7])
        S6 = {"valbuf_free": [], "actT_ready": None, "t_act": None, "t_vals": None, "t_h2": None}
        mT_all = [mT_tok[(15, 3)], mT_tok[(15, 2)], mT_tok[(15, 1)], mT_tok[(15, 0)]]
        t_mt = P.dma('sp', ch_mt, mtile, mT[:, :, 0:512].rearrange("k p n -> p k n"), wait=mT_all)
        t_ax = [P.dma('sp', ch_acc[tt], acc[:, tt, :], t["x"][tt * 128:(tt + 1) * 128, :]) for tt in range(4)]
        out_toks = [None] * 4

        def w_load32(src):
            wi, slot, wfree = ring32.acquire()
            tw = P.dma('pool', wchan32[wi], slot, src.rearrange("(kc p) n -> p kc n", p=128), wait=wfree)
            return wi, slot, tw

        def do_val(j, vb):
            wi, slot, tw = w_load32(t["w_up"][:, j * 2048 + vb * 1024: j * 2048 + (vb + 1) * 1024])
            t_vals = []
            for ci in range(8):
                b, bf = pb2.acquire()
                tp = mm_group(bank(b), [(slot[:, kc, ci * 128:(ci + 1) * 128], h2T[:, kc, :]) for kc in range(NKC)], waits=[tw] + S6["t_h2"] + bf)
                t_vc = op('act', lambda e, b=b, ci=ci: e.activation(out=valbuf[:, ci, :], in_=bank(b), func=AF.Copy),
                          wait=[tp] + S6["valbuf_free"] + [S6["last_mt_read"]])
                bank_free[b] = [t_vc]
                t_vals.append(t_vc)
            ring32.release(wi, [tp])
            S6["t_vals"] = t_vals

        def do_gate(j, vb):
            wi, slot, tw = w_load32(t["w_up"][:, DFF + j * 2048 + vb * 1024: DFF + j * 2048 + (vb + 1) * 1024])
            t_vals = S6["t_vals"]
            for ci in range(8):
                c = vb * 8 + ci
                gc = j * 16 + c
                b, bf = pb2.acquire()
                tp = mm_group(bank(b), [(slot[:, kc, ci * 128:(ci + 1) * 128], h2T[:, kc, :]) for kc in range(NKC)], waits=[tw] + S6["t_h2"] + bf)
                gi, G, gfree = r_G.acquire()
                t_hl = op('dve', lambda e, G=G, gc=gc: e.tensor_copy(out=G[:, 0:2], in_=hl[:, gc, :]), wait=gfree + [t_halo])
                t_gc = op('act', lambda e, G=G, b=b: e.activation(out=G[:, 2:514], in_=bank(b), func=AF.Copy), wait=[tp] + gfree)
                bank_free[b] = [t_gc]
                t_hs = op('dve', lambda e, G=G, gc=gc: e.tensor_copy(out=hl[:, gc, :], in_=G[:, 512:514]), wait=[t_gc, t_hl])
                i2, ca, f2 = r_ca.acquire()
                t_cv = op('dve', lambda e, ca=ca, G=G, gc=gc: e.tensor_scalar(out=ca, in0=G[:, 0:512], scalar1=fc[:, gc, 0:1], scalar2=None, op0=ALU.mult),
                          wait=[t_hl, t_gc] + f2)
                for jj in (1, 2):
                    t_cv = op('dve', lambda e, ca=ca, G=G, gc=gc, jj=jj: e.scalar_tensor_tensor(out=ca, in0=G[:, jj:jj + 512], scalar=fc[:, gc, jj:jj + 1], in1=ca,
                                                                                          op0=ALU.mult, op1=ALU.add), wait=[t_cv])
                r_G.release(gi, [t_cv, t_hs])
                i3, ge, f3 = r_ge2.acquire()
                t_ge = op('act', lambda e, ge=ge, ca=ca, gc=gc: e.activation(out=ge, in_=ca, func=AF.Gelu_apprx_tanh, bias=fc[:, gc, 3:4], scale=1.0),
                          wait=[t_cv] + f3)
                r_ca.release(i2, [t_ge])
                t_act = op('dve', lambda e, ge=ge, c=c, ci=ci: e.tensor_tensor(out=actT[:, c, :], in0=ge, in1=valbuf[:, ci, :], op=ALU.mult),
                           wait=[t_ge, t_vals[ci]] + ([S6["actT_ready"]] if S6["actT_ready"] is not None else []))
                r_ge2.release(i3, [t_act])
            S6["valbuf_free"] = [t_act]
            S6["t_act"] = t_act
            ring32.release(wi, [tp])

        def do_wdown(j, ti):
            final = (j == 2)
            for blk in range(2):
                wi, slot, tw = w_load32(t["w_down"][j * 2048:(j + 1) * 2048, blk * 1024:(blk + 1) * 1024])
                for tt in range(4):
                    for half in range(2):
                        b, bf = pb2.acquire()
                        tp = mm_group(bank(b), [(actT[:, kc, tt * 128:(tt + 1) * 128], slot[:, kc, half * 512:(half + 1) * 512]) for kc in range(NKC)],
                                      waits=[tw, S6["t_act"]] + bf)
                        c0 = blk * 1024 + half * 512
                        i1, tmp, f1 = r_tmp.acquire()
                        t_e1 = op('dve', lambda e, tmp=tmp, b=b, c0=c0: e.tensor_tensor(out=tmp, in0=bank(b), in1=g2bc[:, c0:c0 + 512], op=ALU.mult),
                                  wait=[tp] + f1)
                        bank_free[b] = [t_e1]
                        t_e2 = op('dve', lambda e, tmp=tmp, tt=tt, c0=c0: e.tensor_tensor(out=acc[:, tt, c0:c0 + 512], in0=acc[:, tt, c0:c0 + 512], in1=tmp, op=ALU.add),
                                  wait=[t_e1])
                        r_tmp.release(i1, [t_e2])
                    if final and blk == 1:
                        r0 = ti * 512 + tt * 128
                        out_toks[tt] = P.dma('sp', ch_out[tt], out[r0:r0 + 128, :], acc[:, tt, :], wait=[t_e2])
                        if ti < 3:
                            t_ax[tt] = P.dma('sp', ch_acc[tt], acc[:, tt, :], t["x"][r0 + 512:r0 + 640, :], wait=[out_toks[tt]])
                ring32.release(wi, [tp])
            S6["actT_ready"] = tp

        for ti in range(4):
            wslots = [w_load32(t["w_out"][:, blk * 1024:(blk + 1) * 1024]) for blk in range(2)]
            x1_toks = [None] * 4
            nf = [None] * 4
            for tt in range(4):
                for blk in range(2):
                    wi, slot, tw = wslots[blk]
                    for half in range(2):
                        b, bf = pb2.acquire()
                        tp = mm_group(bank(b), [(mtile[:, kc, tt * 128:(tt + 1) * 128], slot[:, kc, half * 512:(half + 1) * 512]) for kc in range(NKC)],
                                      waits=[tw, t_mt] + bf)
                        c0 = blk * 1024 + half * 512
                        i1, tmp, f1 = r_tmp.acquire()
                        t_e1 = op('dve', lambda e, tmp=tmp, b=b, c0=c0: e.tensor_tensor(out=tmp, in0=bank(b), in1=g1bc[:, c0:c0 + 512], op=ALU.mult),
                                  wait=[tp, t_gbc] + f1)
                        bank_free[b] = [t_e1]
                        t_e2 = op('dve', lambda e, tmp=tmp, tt=tt, c0=c0: e.tensor_tensor(out=acc[:, tt, c0:c0 + 512], in0=acc[:, tt, c0:c0 + 512], in1=tmp, op=ALU.add),
                                  wait=[t_e1, t_ax[tt]])
                        r_tmp.release(i1, [t_e2])
                x1_toks[tt] = t_e2
                if tt >= 1:
                    S6["t_h2"] = norm_back(nf[tt - 1], A2col, B2col, [t_A2, t_B2], h2T[:, :, (tt - 1) * 128:tt * 128], pair2)
                nf[tt] = norm_front(acc[:, tt, :], [x1_toks[tt]], xn2)
            S6["last_mt_read"] = tp
            for blk in range(2):
                ring32.release(wslots[blk][0], [tp])
            S6["t_h2"] = norm_back(nf[3], A2col, B2col, [t_A2, t_B2], h2T[:, :, 384:512], pair2)
            do_val(0, 0); do_gate(0, 0); do_val(0, 1); do_gate(0, 1)
            do_val(1, 0); do_wdown(0, ti); do_gate(1, 0); do_val(1, 1); do_gate(1, 1)
            do_val(2, 0); do_wdown(1, ti); do_gate(2, 0); do_val(2, 1); do_gate(2, 1)
            if ti < 3:
                nr = slice((ti + 1) * 512, (ti + 2) * 512)
                t_mt = P.dma('sp', ch_mt, mtile, mT[:, :, nr].rearrange("k p n -> p k n"), wait=[S6["t_act"], S6["last_mt_read"]])
            do_wdown(2, ti)
        P.wait_only('sp', out_toks)
        return finish()
    return nc


def _prep_inputs(inp, b):
    f = np.float32

    def colform(v, n):
        return np.ascontiguousarray(v.reshape(n, 128).T).astype(f)
    m = {}
    m["x"] = np.ascontiguousarray(inp["x"][b])
    m["cT"] = colform(inp["c"][b], 16)
    m["w_ada"] = inp["w_ada"][0]
    m["b_ada"] = inp["b_ada"][0].reshape(1, -1)
    m["g1col"] = colform(inp["g_norm1"][0], 16)
    m["g2col"] = colform(inp["g_norm2"][0], 16)
    m["w_in"] = inp["w_in"][0]
    rc = np.zeros((128, 8, 8), f)
    cw = inp["conv_rnn_w"][0]
    for j in range(4):
        rc[:, :, j] = colform(cw[j], 8)
    rc[:, :, 4] = colform(inp["conv_rnn_b"][0], 8)
    rc[:, :, 5] = colform(inp["b_rg_a"][0], 8)
    rc[:, :, 6] = colform(inp["b_rg_x"][0], 8)
    rc[:, :, 7] = colform(inp["lru_lambda"][0], 8)
    m["rnn_cols"] = rc.reshape(128, 64)
    for name, key in (("wa_bd", "w_rg_a"), ("wx_bd", "w_rg_x")):
        w = inp[key][0]
        bd = np.zeros((128, 8, 128), f)
        for c in range(8):
            for hh in range(2):
                bd[hh * 64:(hh + 1) * 64, c, hh * 64:(hh + 1) * 64] = w[2 * c + hh]
        m[name] = bd
    m["gqk"] = np.stack([inp["g_q"][0], inp["g_k"][0]], axis=1).astype(f)
    m["w_proj_rnn"] = inp["w_proj_rnn"][0]
    m["w_proj_sb"] = inp["w_proj_sb"][0]
    m["w_out"] = inp["w_out"][0]
    m["w_up"] = inp["w_up"][0]
    m["w_down"] = inp["w_down"][0]
    fc = np.zeros((128, 48, 4), f)
    fw = inp["conv_ffn_w"][0]
    for j in range(3):
        fc[:, :, j] = colform(fw[j], 48)
    fc[:, :, 3] = colform(inp["conv_ffn_b"][0], 48)
    m["ffn_cols"] = fc.reshape(128, 192)
    cst = np.zeros((128, 4, 128), f)
    cst[:, 0, :] = np.eye(128)
    jj, ss = np.meshgrid(np.arange(128), np.arange(128), indexing="ij")
    cst[:, 1, :] = -1.0 * (jj >= ss)
    cst[:, 2, :] = 1.0
    cst[:, 3, :] = (ss > jj)
    m["consts"] = cst
    return m


def kernel(**inputs):
    inp = {k: np.asarray(v) for k, v in inputs.items()}
    nc = build_nc()
    in_maps = [_prep_inputs(inp, b) for b in range(8)]
    res = run_bass_kernel_spmd(nc, in_maps, core_ids=list(range(8)))
    return np.stack([r["out"] for r in res.results], axis=0).astype(np.float32)
```
